# Optimizing a Trainium2 kernel written in Bass

```python
import jax, jax.numpy as jnp
from jax import lax
import numpy as np


D_MODEL = 2048
BATCH = 16
SEQ = 2048
DEPTH = 4

N_MIXERS = 2
N_LRU_LAYERS = (DEPTH + 1) // 2
N_ATT_LAYERS = DEPTH // 2
D_RNN = 2560
LRU_BLOCKS = 10
LRU_BLOCK_W = D_RNN // LRU_BLOCKS
CONV_W = 4
LRU_C = 8.0
N_HEADS = 16
HEAD_DIM = D_MODEL // N_HEADS
ROT_DIM = HEAD_DIM // 4
ROPE_THETA = 500000.0
DILATED_PAIRS = ((128, 1), (512, 4), (2048, 16))
SUB_BLOCK = 128
D_FF = 4 * D_MODEL
EPS = 1e-6
NEG_INF = -1e30

kernel_name = 'hybrid_rglru_dilated_attn_sqrelu'


def _rmsnorm(x, g):
    x32 = x.astype(jnp.float32)
    y = x32 * lax.rsqrt(jnp.mean(x32 * x32, axis=-1, keepdims=True) + EPS)
    return (y * g.astype(jnp.float32)).astype(x.dtype)


def _mlp(h, w1, w2):
    u = jnp.einsum('bsd,df->bsf', h, w1)
    u = jnp.square(jax.nn.relu(u))
    return jnp.einsum('bsf,fd->bsd', u, w2)


def _block_diag(x, w, b):
    bsz, s, _ = x.shape
    xr = x.reshape(bsz, s, LRU_BLOCKS, LRU_BLOCK_W)
    y = jnp.einsum('bshi,hij->bshj', xr, w) + b
    return y.reshape(bsz, s, D_RNN)


def _rglru_block(h, w_in, conv_w, conv_b, w_a, b_a, w_x, b_x, lam, w_out):
    xg = jnp.einsum('bsd,de->bse', h, w_in)
    xb, gb = xg[..., :D_RNN], xg[..., D_RNN:]
    xb = lax.conv_general_dilated(
        xb, conv_w[:, None, :], window_strides=(1,), padding=[(CONV_W - 1, 0)],
        dimension_numbers=('NWC', 'WIO', 'NWC'), feature_group_count=D_RNN) + conv_b
    r = jax.nn.sigmoid(_block_diag(xb, w_a, b_a).astype(jnp.float32))
    i = jax.nn.sigmoid(_block_diag(xb, w_x, b_x).astype(jnp.float32))
    log_a = -LRU_C * r * jax.nn.softplus(-lam.astype(jnp.float32))
    a = jnp.exp(log_a)
    mult = jnp.sqrt(-jnp.expm1(2.0 * log_a))
    u = xb.astype(jnp.float32) * i * mult

    def step(hprev, inp):
        a_t, u_t = inp
        hn = a_t * hprev + u_t
        return hn, hn

    bsz = h.shape[0]
    h0 = jnp.zeros((bsz, D_RNN), jnp.float32)
    _, hs = lax.scan(step, h0, (jnp.swapaxes(a, 0, 1), jnp.swapaxes(u, 0, 1)))
    y = jnp.swapaxes(hs, 0, 1)
    y = (y * jax.nn.gelu(gb.astype(jnp.float32))).astype(h.dtype)
    return jnp.einsum('bse,ed->bsd', y, w_out)


def _rope_partial(t):
    s = t.shape[1]
    half = ROT_DIM // 2
    inv = ROPE_THETA ** (-jnp.arange(0, ROT_DIM, 2, dtype=jnp.float32) / ROT_DIM)
    ang = jnp.arange(s, dtype=jnp.float32)[:, None] * inv[None, :]
    cos = jnp.cos(ang)[None, :, None, :]
    sin = jnp.sin(ang)[None, :, None, :]
    t32 = t.astype(jnp.float32)
    x1, x2 = t32[..., :half], t32[..., half:ROT_DIM]
    out = jnp.concatenate([x1 * cos - x2 * sin, x2 * cos + x1 * sin, t32[..., ROT_DIM:]], axis=-1)
    return out.astype(t.dtype)


def _dilated_branch(q, k, v, window, dilation):
    bsz, s, h, dh = q.shape
    d = dilation
    w_sub = window // d
    L = s // d
    nblk = -(-L // SUB_BLOCK)
    Lp = nblk * SUB_BLOCK

    def gather(t):
        t = t.reshape(bsz, L, d, h, dh).transpose(0, 2, 3, 1, 4)
        t = jnp.pad(t, ((0, 0), (0, 0), (0, 0), (0, Lp - L), (0, 0)))
        return t.reshape(bsz, d, h, nblk, SUB_BLOCK, dh)

    def band(t):
        prev = jnp.pad(t[:, :, :, :-1], ((0, 0), (0, 0), (0, 0), (1, 0), (0, 0), (0, 0)))
        return jnp.concatenate([prev, t], axis=-2)

    qb = gather(q)
    kw = band(gather(k))
    vw = band(gather(v))
    scores = jnp.einsum('brhnqd,brhnkd->brhnqk', qb, kw).astype(jnp.float32) * (HEAD_DIM ** -0.5)
    q_idx = jnp.arange(nblk)[:, None, None] * SUB_BLOCK + jnp.arange(SUB_BLOCK)[None, :, None]
    k_idx = (jnp.arange(nblk)[:, None, None] - 1) * SUB_BLOCK + jnp.arange(2 * SUB_BLOCK)[None, None, :]
    dist = q_idx - k_idx
    mask = (dist >= 0) & (dist <= w_sub) & (k_idx >= 0)
    scores = jnp.where(mask, scores, NEG_INF)
    m = jnp.max(scores, axis=-1, keepdims=True)
    p = jnp.exp(scores - m)
    denom = jnp.sum(p, axis=-1)
    o = jnp.einsum('brhnqk,brhnkd->brhnqd', p.astype(v.dtype), vw).astype(jnp.float32)
    o = o / denom[..., None]
    lse = m[..., 0] + jnp.log(denom)
    o = o.reshape(bsz, d, h, Lp, dh)[:, :, :, :L].transpose(0, 3, 1, 2, 4).reshape(bsz, s, h, dh)
    lse = lse.reshape(bsz, d, h, Lp)[..., :L].transpose(0, 3, 1, 2).reshape(bsz, s, h)
    return o, lse


def _dilated_attention_block(h, w_qkv, w_o):
    bsz, s, _ = h.shape
    qkv = jnp.einsum('bsd,de->bse', h, w_qkv).reshape(bsz, s, 3, N_HEADS, HEAD_DIM)
    q = _rope_partial(qkv[:, :, 0])
    k = _rope_partial(qkv[:, :, 1])
    v = qkv[:, :, 2]
    outs, lses = [], []
    for window, dilation in DILATED_PAIRS:
        o_g, lse_g = _dilated_branch(q, k, v, window, dilation)
        outs.append(o_g)
        lses.append(lse_g)
    outs = jnp.stack(outs, axis=0)
    wts = jax.nn.softmax(jnp.stack(lses, axis=0), axis=0)
    o = jnp.sum(wts[..., None] * outs, axis=0).astype(h.dtype)
    return jnp.einsum('bse,ed->bsd', o.reshape(bsz, s, D_MODEL), w_o)


def setup_inputs(seed: int = 0) -> dict:
    key = jax.random.key(seed)
    ks = jax.random.split(key, 20)
    f32 = jnp.float32
    nrm = lambda k, shape, scale: jax.random.normal(k, shape, f32) * scale
    x = jax.random.normal(ks[0], (BATCH, SEQ, D_MODEL), f32)
    mix_norm = 1.0 + nrm(ks[1], (DEPTH, D_MODEL), 0.02)
    mlp_norm = 1.0 + nrm(ks[2], (DEPTH, D_MODEL), 0.02)
    final_norm = 1.0 + nrm(ks[3], (D_MODEL,), 0.02)
    mlp_w1 = nrm(ks[4], (DEPTH, D_MODEL, D_FF), D_MODEL ** -0.5)
    mlp_w2 = nrm(ks[5], (DEPTH, D_FF, D_MODEL), D_FF ** -0.5)
    lru_w_in = nrm(ks[6], (N_LRU_LAYERS, D_MODEL, 2 * D_RNN), D_MODEL ** -0.5)
    lru_conv_w = nrm(ks[7], (N_LRU_LAYERS, CONV_W, D_RNN), CONV_W ** -0.5)
    lru_conv_b = nrm(ks[8], (N_LRU_LAYERS, D_RNN), 0.01)
    lru_w_a = nrm(ks[9], (N_LRU_LAYERS, LRU_BLOCKS, LRU_BLOCK_W, LRU_BLOCK_W), LRU_BLOCK_W ** -0.5)
    lru_b_a = nrm(ks[10], (N_LRU_LAYERS, LRU_BLOCKS, LRU_BLOCK_W), 0.01)
    lru_w_x = nrm(ks[11], (N_LRU_LAYERS, LRU_BLOCKS, LRU_BLOCK_W, LRU_BLOCK_W), LRU_BLOCK_W ** -0.5)
    lru_b_x = nrm(ks[12], (N_LRU_LAYERS, LRU_BLOCKS, LRU_BLOCK_W), 0.01)
    a_c = jax.random.uniform(ks[13], (N_LRU_LAYERS, D_RNN), f32, 0.9, 0.999)
    a0 = a_c ** (1.0 / LRU_C)
    lru_lambda = jnp.log(a0) - jnp.log1p(-a0)
    lru_w_out = nrm(ks[14], (N_LRU_LAYERS, D_RNN, D_MODEL), D_RNN ** -0.5)
    attn_w_qkv = nrm(ks[15], (N_ATT_LAYERS, D_MODEL, 3 * D_MODEL), D_MODEL ** -0.5)
    attn_w_o = nrm(ks[16], (N_ATT_LAYERS, D_MODEL, D_MODEL), D_MODEL ** -0.5)
    return {'x': x, 'mix_norm': mix_norm, 'mlp_norm': mlp_norm, 'final_norm': final_norm,
            'mlp_w1': mlp_w1, 'mlp_w2': mlp_w2,
            'lru_w_in': lru_w_in, 'lru_conv_w': lru_conv_w, 'lru_conv_b': lru_conv_b,
            'lru_w_a': lru_w_a, 'lru_b_a': lru_b_a, 'lru_w_x': lru_w_x, 'lru_b_x': lru_b_x,
            'lru_lambda': lru_lambda, 'lru_w_out': lru_w_out,
            'attn_w_qkv': attn_w_qkv, 'attn_w_o': attn_w_o}


def reference(x, mix_norm, mlp_norm, final_norm, mlp_w1, mlp_w2,
              lru_w_in, lru_conv_w, lru_conv_b, lru_w_a, lru_b_a, lru_w_x, lru_b_x,
              lru_lambda, lru_w_out, attn_w_qkv, attn_w_o):
    for i in range(DEPTH):
        h = _rmsnorm(x, mix_norm[i])
        j = i // N_MIXERS
        if i % N_MIXERS == 0:
            x = x + _rglru_block(h, lru_w_in[j], lru_conv_w[j], lru_conv_b[j],
                                 lru_w_a[j], lru_b_a[j], lru_w_x[j], lru_b_x[j],
                                 lru_lambda[j], lru_w_out[j])
        else:
            x = x + _dilated_attention_block(h, attn_w_qkv[j], attn_w_o[j])
        x = x + _mlp(_rmsnorm(x, mlp_norm[i]), mlp_w1[i], mlp_w2[i])
    return _rmsnorm(x, final_norm)
```

```python
import contextlib
import os
import numpy as np
import ml_dtypes
import concourse.bass as bass
import concourse.mybir as mybir
from concourse.bass_utils import run_bass_kernel_spmd

F32 = mybir.dt.float32
BF16 = mybir.dt.bfloat16
AF = mybir.ActivationFunctionType
ALU = mybir.AluOpType

D = 2048
DC = 16
S = 2048
FF = 8192
DR = 2560
RC = 20
NH = 16
EPS = 1e-6
ENGS = ("sync", "scalar", "gpsimd", "vector", "tensor")


class Prog:
    def __init__(self, nc, stack):
        self.nc = nc
        self.stack = stack
        self.ops = {e: [] for e in ENGS}
        self.sem = {}
        self.cnt = {}
        self.last_w = {}
        self.readers = {}
        self.final_waits = {}
        for e in ("scalar", "gpsimd", "vector", "tensor"):
            self.newsem("E_" + e)

    def newsem(self, name):
        if name not in self.sem:
            self.sem[name] = self.stack.enter_context(self.nc.semaphore(name))
            self.cnt[name] = 0

    def op(self, eng, fn, reads=(), writes=(), dsem=None, sig=True):
        waits = {}
        own = "E_" + eng

        def addw(tok):
            if tok is None:
                return
            s, v = tok
            if s == own and eng == "tensor":
                return
            if waits.get(s, 0) < v:
                waits[s] = v

        for k in reads:
            addw(self.last_w.get(k))
        for k in writes:
            addw(self.last_w.get(k))
            for r in self.readers.get(k, ()):
                addw(r)
        if dsem is not None:
            self.newsem(dsem)
            sname, n = dsem, 16
            self.cnt[sname] += n
            tok = (sname, self.cnt[sname])
        elif sig:
            sname, n = own, 1
            self.cnt[sname] += 1
            tok = (sname, self.cnt[sname])
        else:
            sname, n = None, 0
            tok = (own, self.cnt[own] + 1)
        self.ops[eng].append((fn, waits, sname, n))
        for k in reads:
            self.readers.setdefault(k, []).append(tok)
        for k in writes:
            self.last_w[k] = tok
            self.readers[k] = []
        return tok

    def barrier(self):
        allw = {s: v for s, v in self.cnt.items() if v > 0}
        for e in ENGS:
            self.ops[e].append((None, dict(allw), None, 0))
        self.last_w = {}
        self.readers = {}

    def emit(self, block):
        prog = self

        def mk(ename):
            def body(eng):
                seen = {}
                for fn, waits, sname, n in prog.ops[ename]:
                    for s, v in sorted(waits.items()):
                        if seen.get(s, 0) < v:
                            eng.wait_ge(prog.sem[s], v)
                            seen[s] = v
                    if fn is None:
                        continue
                    ins = fn(eng)
                    if sname is not None:
                        ins.then_inc(prog.sem[sname], n)
                for s, v in sorted(prog.final_waits.get(ename, {}).items()):
                    if seen.get(s, 0) < v:
                        eng.wait_ge(prog.sem[s], v)
            return body

        for ename in ENGS:
            if prog.ops[ename] or prog.final_waits.get(ename):
                getattr(block, ename)(mk(ename))


class Env:
    pass


def build(nseq, plan, final_norm=True):
    NTOK = nseq * S
    nc = bass.Bass("TRN2", target_bir_lowering=False)
    dt_in = lambda name, shape, dt=F32: nc.dram_tensor(name, list(shape), dt, kind="ExternalInput").ap()
    x_in = dt_in("x", [NTOK, D])
    gains = dt_in("gains", [9, 128, D])
    w1 = dt_in("mlp_w1", [4, D, FF])
    w2 = dt_in("mlp_w2", [4, FF, D])
    w_in = dt_in("lru_w_in", [2, D, 2 * DR])
    lruvec = dt_in("lruvec", [2, 128, 8, RC])
    w_a = dt_in("lru_w_a", [2, 10, 256, 256])
    w_x = dt_in("lru_w_x", [2, 10, 256, 256])
    w_out = dt_in("lru_w_out", [2, DR, D])
    w_qkv = dt_in("attn_w_qkv", [2, D, 3 * D])
    w_o = dt_in("attn_w_o", [2, D, D])
    cst = dt_in("cst", [128, 640], BF16)
    rope = dt_in("rope", [2, 128, S])
    out = nc.dram_tensor("out", [NTOK, D], F32, kind="ExternalOutput").ap()
    xres = nc.dram_tensor("xres", [NTOK, D], F32).ap()
    qsc = nc.dram_tensor("qsc", [nseq, NH, 128, S], BF16).ap()
    ksc = nc.dram_tensor("ksc", [nseq, NH, 128, S], BF16).ap()
    vsc = nc.dram_tensor("vsc", [NTOK, D], BF16).ap()
    osc = nc.dram_tensor("osc", [nseq, D, S], BF16).ap()

    with contextlib.ExitStack() as st:
        P = Prog(nc, st)
        cs = st.enter_context(nc.sbuf_tensor("cs", [128, 640], BF16))
        P.op("sync", lambda e: e.dma_start(out=cs[:], in_=cst), writes=["cs"], dsem="D_cs")
        ident = cs[:, 0:128]
        ones = cs[:, 128:256]
        mask2 = cs[:, 256:512]
        Rt = cs[:, 512:640]

        uid = [0]

        def sbt(stack, name, shape, dt):
            uid[0] += 1
            return stack.enter_context(nc.sbuf_tensor(f"{name}_u{uid[0]}", list(shape), dt))

        def pst(stack, name, shape, dt=F32):
            uid[0] += 1
            return stack.enter_context(nc.psum_tensor(f"{name}_u{uid[0]}", list(shape), dt))

        def norm_tile(ev, xap, xkey, gap, hT, hkey, col0, i, res_scale=None):
            norm_pre(ev, xap, xkey, gap, i)
            norm_T(ev, hT, hkey, col0, i)

        def norm_pre(ev, xap, xkey, gap, i):
            sl = i % 2
            ss = ev.ss[:, sl:sl + 1]
            hb = ev.hb[sl]
            P.op("scalar", lambda e: e.activation(out=hb[:], in_=xap, func=AF.Square, accum_out=ss),
                 reads=[xkey], writes=[f"hb{sl}", f"ss{sl}"])
            P.op("scalar", lambda e: e.activation(out=ss, in_=ss, func=AF.Sqrt, scale=1.0 / D, bias=ev.epsb[:, 0:1]),
                 reads=[f"ss{sl}"], writes=[f"ss{sl}"])
            P.op("vector", lambda e: e.reciprocal(out=ss, in_=ss), reads=[f"ss{sl}"], writes=[f"ss{sl}"])
            P.op("vector", lambda e: e.scalar_tensor_tensor(out=hb[:], in0=xap, scalar=ss, in1=gap, op0=ALU.mult, op1=ALU.mult),
                 reads=[xkey, f"ss{sl}", "g"], writes=[f"hb{sl}"])

        def norm_T(ev, hT, hkey, col0, i):
            sl = i % 2
            hb = ev.hb[sl]
            if getattr(ev, "psTf", None) is not None:
                for q in range(4):
                    bank, bkey = ev.psTf[q]
                    for c in range(4):
                        cc = q * 4 + c
                        P.op("tensor", lambda e, bank=bank, c=c, cc=cc: e.matmul(
                            bank[:, c * 128:(c + 1) * 128], lhsT=hb[:, cc * 128:(cc + 1) * 128], rhs=ident, start=True, stop=True),
                            reads=[f"hb{sl}", "cs"], writes=[bkey], sig=(c == 3))
                    if q % 2 == 0:
                        P.op("scalar", lambda e, bank=bank, q=q: e.copy(
                            out=hT[:, q * 4:(q + 1) * 4, col0:col0 + 128], in_=bank[:].rearrange("p (a b) -> p a b", a=4)),
                            reads=[bkey], writes=[hkey])
                    else:
                        P.op("vector", lambda e, bank=bank, q=q: e.tensor_copy(
                            out=hT[:, q * 4:(q + 1) * 4, col0:col0 + 128], in_=bank[:].rearrange("p (a b) -> p a b", a=4)),
                            reads=[bkey], writes=[hkey])
                return
            for hf in range(2):
                pT = ev.psT[hf]
                for c in range(8):
                    cc = hf * 8 + c
                    P.op("tensor", lambda e, pT=pT, c=c, cc=cc: e.transpose(out=pT[:, c, :], in_=hb[:, cc * 128:(cc + 1) * 128], identity=ident),
                         reads=[f"hb{sl}", "cs"], writes=[f"psT{hf}"], sig=(c == 7))
                eng = "scalar" if hf == 0 else "vector"
                if eng == "scalar":
                    P.op("scalar", lambda e, pT=pT, hf=hf: e.copy(out=hT[:, hf * 8:(hf + 1) * 8, col0:col0 + 128], in_=pT[:]),
                         reads=[f"psT{hf}"], writes=[hkey])
                else:
                    P.op("vector", lambda e, pT=pT, hf=hf: e.tensor_copy(out=hT[:, hf * 8:(hf + 1) * 8, col0:col0 + 128], in_=pT[:]),
                         reads=[f"psT{hf}"], writes=[hkey])

        def norm_bufs(ev, ps, gi, own_psT=True):
            ev.hb = [sbt(ps, f"hb{i}", [128, D], BF16) for i in range(2)]
            ev.ss = sbt(ps, "ss", [128, 2], F32)
            ev.epsb = sbt(ps, "epsb", [128, 1], F32)
            ev.g = sbt(ps, "g", [128, D], F32)
            ev.psTf = None
            if own_psT:
                ev.psT = [pst(ps, f"psT{i}", [128, 8, 128], BF16) for i in range(2)]
            P.op("vector", lambda e: e.memset(ev.epsb[:], EPS), writes=["epsb"])
            P.op("sync", lambda e: e.dma_start(out=ev.g[:], in_=gains[gi]), writes=["g"], dsem="D_g")

        def phase_mlp(li, gi, src, dst, fin):
            TT, NT = 1024, 8
            with contextlib.ExitStack() as ps:
                ev = Env()
                norm_bufs(ev, ps, gi)
                xacc = sbt(ps, "xacc", [128, NT, D], F32)
                hT = sbt(ps, "hT", [128, DC, TT], BF16)
                uT = [sbt(ps, f"uT{i}", [128, 8, TT], BF16) for i in range(2)]
                w1s = [sbt(ps, f"w1s{i}", [128, DC, 256], BF16) for i in range(2)]
                w2s = [sbt(ps, f"w2s{i}", [128, 8, 512], BF16) for i in range(4)]
                rt = [sbt(ps, f"rt{i}", [128, 512], F32) for i in range(2)]
                psA = [pst(ps, f"psA{i}", [128, 512]) for i in range(2)]
                psB = [pst(ps, f"psB{i}", [128, 512]) for i in range(4)]
                ssfin = sbt(ps, "ssfin", [128, 2], F32)
                if fin:
                    gf = sbt(ps, "gf", [128, D], F32)
                    P.op("sync", lambda e: e.dma_start(out=gf[:], in_=gains[8]), writes=["gf"], dsem="D_gf")
                cnt = {"w1": 0, "w2": 0, "a": 0, "b": 0, "r": 0}
                NPASS = NTOK // TT

                def load_tile(t, r0):
                    P.op("sync", lambda e, t=t, r0=r0: e.dma_start(out=xacc[:, t, :], in_=src[r0 + t * 128:r0 + (t + 1) * 128, :]),
                         writes=[f"xacc{t}"], dsem=f"D_x{t}")

                def finish_tile(t, r0):
                    if fin:
                        sl = t % 2
                        ss = ev.ss[:, sl:sl + 1]
                        ssf = ssfin[:, sl:sl + 1]
                        P.op("scalar", lambda e, t=t, ssf=ssf: e.activation(
                            out=uT[0][:, 0:2, :], in_=xacc[:, t, :].rearrange("p (a b) -> p a b", a=2), func=AF.Square, accum_out=ssf),
                            reads=[f"xacc{t}"], writes=["uT0h0", "uT0h1", f"ssf{sl}"])
                        P.op("scalar", lambda e, ssf=ssf: e.activation(out=ssf, in_=ssf, func=AF.Sqrt, scale=1.0 / D, bias=ev.epsb[:, 0:1]),
                             reads=[f"ssf{sl}"], writes=[f"ssf{sl}"])
                        P.op("vector", lambda e, ssf=ssf: e.reciprocal(out=ssf, in_=ssf), reads=[f"ssf{sl}"], writes=[f"ssf{sl}"])
                        P.op("vector", lambda e, t=t, ssf=ssf: e.scalar_tensor_tensor(
                            out=xacc[:, t, :], in0=xacc[:, t, :], scalar=ssf, in1=gf[:], op0=ALU.mult, op1=ALU.mult),
                            reads=[f"ssf{sl}", "gf"], writes=[f"xacc{t}"])
                    P.op("sync", lambda e, t=t, r0=r0: e.dma_start(out=dst[r0 + t * 128:r0 + (t + 1) * 128, :], in_=xacc[:, t, :]),
                         reads=[f"xacc{t}"], writes=[f"dst{r0 + t * 128}"], dsem=f"D_xs{t}")

                for p in range(NPASS):
                    r0 = p * TT
                    if p == 0:
                        for t in range(NT):
                            load_tile(t, r0)
                        for t in range(NT):
                            norm_tile(ev, xacc[:, t, :], f"xacc{t}", ev.g[:], hT, f"hT{t}", t * 128, t)

                    def stepA(gidx):
                        sl = gidx % 2
                        for pc in range(4):
                            ws = cnt["w1"] % 2
                            cnt["w1"] += 1
                            f0 = (gidx * 8 + pc * 2) * 128
                            P.op("gpsimd", lambda e, ws=ws, f0=f0: e.dma_start(
                                out=w1s[ws][:], in_=w1[li][:, f0:f0 + 256].rearrange("(c p) f -> p c f", p=128)),
                                writes=[f"w1s{ws}"], dsem=f"D_w1{ws}")
                            for j in range(2):
                                fc = pc * 2 + j
                                for hf in range(2):
                                    bk = cnt["a"] % 2
                                    cnt["a"] += 1
                                    for k in range(DC):
                                        P.op("tensor", lambda e, bk=bk, ws=ws, k=k, j=j, hf=hf: e.matmul(
                                            psA[bk][:], lhsT=w1s[ws][:, k, j * 128:(j + 1) * 128], rhs=hT[:, k, hf * 512:(hf + 1) * 512],
                                            start=(k == 0), stop=(k == DC - 1)),
                                            reads=[f"w1s{ws}"] + [f"hT{t}" for t in range(hf * 4, hf * 4 + 4)],
                                            writes=[f"psA{bk}"], sig=(k == DC - 1))
                                    ri = cnt["r"] % 2
                                    cnt["r"] += 1
                                    P.op("scalar", lambda e, bk=bk, ri=ri: e.activation(out=rt[ri][:], in_=psA[bk][:], func=AF.Relu),
                                         reads=[f"psA{bk}"], writes=[f"rt{ri}"])
                                    P.op("vector", lambda e, ri=ri, sl=sl, fc=fc, hf=hf: e.tensor_tensor(
                                        out=uT[sl][:, fc, hf * 512:(hf + 1) * 512], in0=rt[ri][:], in1=rt[ri][:], op=ALU.mult),
                                        reads=[f"rt{ri}"], writes=[f"uT{sl}h{hf}"])

                    def stepB(gidx):
                        sl = gidx % 2
                        for n in range(4):
                            ws = cnt["w2"] % 4
                            cnt["w2"] += 1
                            P.op("gpsimd", lambda e, ws=ws, n=n: e.dma_start(
                                out=w2s[ws][:], in_=w2[li][gidx * 1024:(gidx + 1) * 1024, n * 512:(n + 1) * 512].rearrange("(c p) d -> p c d", p=128)),
                                writes=[f"w2s{ws}"], dsem=f"D_w2{ws}")
                            for t in range(NT):
                                bk = cnt["b"] % 4
                                cnt["b"] += 1
                                for k in range(8):
                                    P.op("tensor", lambda e, bk=bk, ws=ws, k=k, t=t: e.matmul(
                                        psB[bk][:], lhsT=uT[sl][:, k, t * 128:(t + 1) * 128], rhs=w2s[ws][:, k, :],
                                        start=(k == 0), stop=(k == 7)),
                                        reads=[f"w2s{ws}", f"uT{sl}h{t // 4}"], writes=[f"psB{bk}"], sig=(k == 7))
                                P.op("vector", lambda e, bk=bk, t=t, n=n: e.tensor_tensor(
                                    out=xacc[:, t, n * 512:(n + 1) * 512], in0=psB[bk][:], in1=xacc[:, t, n * 512:(n + 1) * 512], op=ALU.add),
                                    reads=[f"psB{bk}"], writes=[f"xacc{t}"])

                    def stepB_last(gidx, prefetch):
                        sl = gidx % 2
                        wsl = []
                        for n in range(4):
                            ws = cnt["w2"] % 4
                            cnt["w2"] += 1
                            P.op("gpsimd", lambda e, ws=ws, n=n: e.dma_start(
                                out=w2s[ws][:], in_=w2[li][gidx * 1024:(gidx + 1) * 1024, n * 512:(n + 1) * 512].rearrange("(c p) d -> p c d", p=128)),
                                writes=[f"w2s{ws}"], dsem=f"D_w2{ws}")
                            wsl.append(ws)
                        for t in range(NT):
                            for n in range(4):
                                ws = wsl[n]
                                bk = cnt["b"] % 4
                                cnt["b"] += 1
                                for k in range(8):
                                    P.op("tensor", lambda e, bk=bk, ws=ws, k=k, t=t: e.matmul(
                                        psB[bk][:], lhsT=uT[sl][:, k, t * 128:(t + 1) * 128], rhs=w2s[ws][:, k, :],
                                        start=(k == 0), stop=(k == 7)),
                                        reads=[f"w2s{ws}", f"uT{sl}h{t // 4}"], writes=[f"psB{bk}"], sig=(k == 7))
                                P.op("vector", lambda e, bk=bk, t=t, n=n: e.tensor_tensor(
                                    out=xacc[:, t, n * 512:(n + 1) * 512], in0=psB[bk][:], in1=xacc[:, t, n * 512:(n + 1) * 512], op=ALU.add),
                                    reads=[f"psB{bk}"], writes=[f"xacc{t}"])
                            finish_tile(t, r0)
                            if prefetch:
                                load_tile(t, r0 + TT)
                                if t >= 1:
                                    norm_pre(ev, xacc[:, t - 1, :], f"xacc{t - 1}", ev.g[:], t - 1)
                                if t >= 2:
                                    norm_T(ev, hT, f"hT{t - 2}", (t - 2) * 128, t - 2)
                        if prefetch:
                            norm_pre(ev, xacc[:, NT - 1, :], f"xacc{NT - 1}", ev.g[:], NT - 1)
                            norm_T(ev, hT, f"hT{NT - 2}", (NT - 2) * 128, NT - 2)
                            norm_T(ev, hT, f"hT{NT - 1}", (NT - 1) * 128, NT - 1)

                    stepA(0)
                    for gidx in range(8):
                        if gidx + 1 < 8:
                            stepA(gidx + 1)
                        if gidx < 7:
                            stepB(gidx)
                        else:
                            stepB_last(gidx, p + 1 < NPASS)
                P.barrier()

        def phase_lru(lj, gi, src, dst):
            TT, NT = 512, 4
            with contextlib.ExitStack() as ps:
                ev = Env()
                norm_bufs(ev, ps, gi, own_psT=False)
                xacc = sbt(ps, "xacc", [128, NT, D], F32)
                hT = sbt(ps, "hT", [128, DC, TT], BF16)
                yT = sbt(ps, "yT", [128, RC, TT], BF16)
                wis = [sbt(ps, f"wis{i}", [128, DC, 256], BF16) for i in range(4)]
                wg = [sbt(ps, f"wg{i}", [128, 10, 2, 256], BF16) for i in range(2)]
                wos = [sbt(ps, f"wos{i}", [128, RC, 256], BF16) for i in range(2)]
                vec = sbt(ps, "vec", [128, 8, RC], F32)
                nsp = sbt(ps, "nsp", [128, RC], F32)
                nsp2 = sbt(ps, "nsp2", [128, RC], F32)
                hst = sbt(ps, "hst", [128, RC], F32)
                halo = sbt(ps, "halo", [128, RC, 4], F32)
                tA = sbt(ps, "tA", [128, 2, 516], F32)
                tB = [sbt(ps, f"tB{i}", [128, 2, 512], F32) for i in range(2)]
                tH = [sbt(ps, f"tH{i}", [128, 2, 512], F32) for i in range(2)]
                tD = sbt(ps, "tD", [128, 2, 512], F32)
                tE = sbt(ps, "tE", [128, 2, 512], F32)
                tF = sbt(ps, "tF", [128, 2, 512], F32)
                tG = sbt(ps, "tG", [128, 2, 512], F32)
                tI = sbt(ps, "tI", [128, 2, 512], F32)
                tC = sbt(ps, "tC", [128, 2, 512], BF16)
                psW = pst(ps, "psW", [128, 4, 512])
                psG = [pst(ps, f"psG{i}", [128, 512]) for i in range(4)]
                ev.psTf = [(psG[i], f"psG{i}") for i in range(4)]
                P.op("sync", lambda e: e.dma_start(out=vec[:], in_=lruvec[lj]), writes=["vec"], dsem="D_vec")
                for gt, wsrc in enumerate((w_a, w_x)):
                    for b in range(10):
                        P.op("gpsimd", lambda e, gt=gt, b=b, wsrc=wsrc: e.dma_start(
                            out=wg[gt][:, b, :, :], in_=wsrc[lj, b].rearrange("(c p) n -> p c n", p=128)),
                            writes=[f"wg{gt}"], dsem=f"D_wg{gt}")
                P.op("scalar", lambda e: e.activation(out=nsp[:], in_=vec[:, 7, :], func=AF.Exp, scale=-1.0), reads=["vec"], writes=["nsp"])
                P.op("scalar", lambda e: e.activation(out=nsp[:], in_=nsp[:], func=AF.Ln, bias=1.0), reads=["nsp"], writes=["nsp"])
                P.op("vector", lambda e: e.tensor_scalar(out=nsp2[:], in0=nsp[:], scalar1=-16.0, scalar2=None, op0=ALU.mult), reads=["nsp"], writes=["nsp2"])
                P.op("vector", lambda e: e.tensor_scalar(out=nsp[:], in0=nsp[:], scalar1=-8.0, scalar2=None, op0=ALU.mult), reads=["nsp", "nsp2"], writes=["nsp"])
                cnt = {"wi": 0, "g": 0, "wo": 0, "w": 0}

                def load_block_w(b):
                    res = []
                    for part in range(2):
                        ws = cnt["wi"] % 4
                        cnt["wi"] += 1
                        c0 = part * DR + b * 256
                        P.op("gpsimd", lambda e, ws=ws, c0=c0: e.dma_start(
                            out=wis[ws][:], in_=w_in[lj][:, c0:c0 + 256].rearrange("(c p) f -> p c f", p=128)),
                            writes=[f"wis{ws}"], dsem=f"D_wi{ws}")
                        res.append(ws)
                    return res

                for sq in range(nseq):
                    P.op("vector", lambda e: e.memset(hst[:], 0.0), writes=["hst"])
                    P.op("vector", lambda e: e.memset(halo[:], 0.0), writes=["halo"])
                    for ch in range(S // TT):
                        r0 = sq * S + ch * TT
                        for t in range(NT):
                            P.op("sync", lambda e, t=t, r0=r0: e.dma_start(out=xacc[:, t, :], in_=src[r0 + t * 128:r0 + (t + 1) * 128, :]),
                                 writes=[f"xacc{t}"], dsem=f"D_x{t}")
                        wslots = {0: load_block_w(0)}
                        for t in range(NT):
                            norm_tile(ev, xacc[:, t, :], f"xacc{t}", ev.g[:], hT, "hT", t * 128, t)

                        def wmm(b):
                            slots = wslots[b]
                            for part in range(2):
                                ws = slots[part]
                                for jj in range(2):
                                    bk = part * 2 + jj
                                    for k in range(DC):
                                        P.op("tensor", lambda e, bk=bk, ws=ws, k=k, jj=jj: e.matmul(
                                            psW[:, bk, :], lhsT=wis[ws][:, k, jj * 128:(jj + 1) * 128], rhs=hT[:, k, :],
                                            start=(k == 0), stop=(k == DC - 1)),
                                            reads=[f"wis{ws}", "hT"], writes=["psWx" if part == 0 else "psWg"], sig=(k == DC - 1))

                        def ev_(b):
                            pb = b % 2
                            c0 = 2 * b
                            Ht = tH[pb]
                            P.op("vector", lambda e: e.tensor_copy(out=tA[:, :, 0:3], in_=halo[:, c0:c0 + 2, 0:3]), reads=["halo"], writes=["tA"])
                            P.op("scalar", lambda e: e.copy(out=tA[:, :, 3:515], in_=psW[:, 0:2, :]), reads=["psWx"], writes=["tA"])
                            P.op("scalar", lambda e: e.copy(out=Ht[:], in_=psW[:, 2:4, :]), reads=["psWg"], writes=[f"tH{pb}"])
                            P.op("vector", lambda e: e.tensor_copy(out=halo[:, c0:c0 + 2, 0:3], in_=tA[:, :, 512:515]), reads=["tA"], writes=["halo"])

                        def conv_(b):
                            pb = b % 2
                            c0 = 2 * b
                            Bt = tB[pb]
                            for jj in range(2):
                                c = c0 + jj
                                P.op("vector", lambda e, jj=jj, c=c: e.tensor_scalar(
                                    out=Bt[:, jj, :], in0=tA[:, jj, 3:515], scalar1=vec[:, 3, c:c + 1], scalar2=vec[:, 4, c:c + 1],
                                    op0=ALU.mult, op1=ALU.add),
                                    reads=["tA", "vec"], writes=[f"tB{pb}j{jj}"])
                            for k in range(3):
                                for jj in range(2):
                                    c = c0 + jj
                                    P.op("vector", lambda e, jj=jj, c=c, k=k: e.scalar_tensor_tensor(
                                        out=Bt[:, jj, :], in0=tA[:, jj, k:k + 512], scalar=vec[:, k, c:c + 1], in1=Bt[:, jj, :], op0=ALU.mult, op1=ALU.add),
                                        reads=["tA", "vec"], writes=[f"tB{pb}j{jj}"])
                            P.op("vector", lambda e: e.tensor_copy(out=tC[:], in_=Bt[:]), reads=[f"tB{pb}j0", f"tB{pb}j1"], writes=["tC"])

                        def gelupre_(b):
                            pb = b % 2
                            Ht = tH[pb]
                            P.op("vector", lambda e: e.tensor_tensor(out=tI[:], in0=Ht[:], in1=Ht[:], op=ALU.mult), reads=[f"tH{pb}"], writes=["tI"])
                            P.op("vector", lambda e: e.tensor_scalar(out=tI[:], in0=tI[:], scalar1=0.044715, scalar2=1.0, op0=ALU.mult, op1=ALU.add),
                                 reads=["tI"], writes=["tI"])
                            P.op("vector", lambda e: e.tensor_tensor(out=tI[:], in0=tI[:], in1=Ht[:], op=ALU.mult), reads=[f"tH{pb}"], writes=["tI"])

                        def gates(b, gt):
                            if True:
                                for jo in range(2):
                                    bk = cnt["g"] % 4
                                    cnt["g"] += 1
                                    c = 2 * b + jo
                                    for jj in range(2):
                                        P.op("tensor", lambda e, bk=bk, gt=gt, jo=jo, jj=jj: e.matmul(
                                            psG[bk][:], lhsT=wg[gt][:, b, jj, jo * 128:(jo + 1) * 128], rhs=tC[:, jj, :],
                                            start=(jj == 0), stop=(jj == 1)),
                                            reads=[f"wg{gt}", "tC"], writes=[f"psG{bk}"], sig=(jj == 1))
                                    dstt = tD if gt == 0 else tE
                                    nm = "D" if gt == 0 else "E"
                                    P.op("scalar", lambda e, bk=bk, dstt=dstt, gt=gt, c=c, jo=jo: e.activation(
                                        out=dstt[:, jo, :], in_=psG[bk][:], func=AF.Sigmoid, bias=vec[:, 5 + gt, c:c + 1]),
                                        reads=[f"psG{bk}", "vec"], writes=[f"t{nm}j{jo}"])

                        def st2(b):
                            pb = b % 2
                            c0 = 2 * b
                            Bt, Ht = tB[pb], tH[pb]
                            for jo in range(2):
                                c = c0 + jo
                                P.op("scalar", lambda e, jo=jo, c=c: e.activation(out=tF[:, jo, :], in_=tD[:, jo, :], func=AF.Exp, scale=nsp2[:, c:c + 1]),
                                     reads=[f"tDj{jo}", "nsp2"], writes=[f"tFj{jo}"])
                            for jo in range(2):
                                c = c0 + jo
                                P.op("scalar", lambda e, jo=jo, c=c: e.activation(out=tD[:, jo, :], in_=tD[:, jo, :], func=AF.Exp, scale=nsp[:, c:c + 1]),
                                     reads=[f"tDj{jo}", "nsp"], writes=[f"tDj{jo}"])
                            P.op("vector", lambda e: e.tensor_scalar(out=tF[:], in0=tF[:], scalar1=1.0, scalar2=None, op0=ALU.min),
                                 reads=["tFj0", "tFj1"], writes=["tFj0", "tFj1"])
                            P.op("scalar", lambda e: e.activation(out=tF[:], in_=tF[:], func=AF.Sqrt, scale=-1.0, bias=1.0),
                                 reads=["tFj0", "tFj1"], writes=["tFj0", "tFj1"])
                            P.op("scalar", lambda e: e.activation(out=tI[:], in_=tI[:], func=AF.Sigmoid, scale=1.5957691), reads=["tI"], writes=["tI"])
                            P.op("vector", lambda e: e.tensor_tensor(out=Bt[:], in0=Bt[:], in1=tE[:], op=ALU.mult),
                                 reads=["tEj0", "tEj1"], writes=[f"tB{pb}j0", f"tB{pb}j1"])
                            P.op("vector", lambda e: e.tensor_tensor(out=Bt[:], in0=Bt[:], in1=tF[:], op=ALU.mult),
                                 reads=["tFj0", "tFj1"], writes=[f"tB{pb}j0", f"tB{pb}j1"])
                            for jo in range(2):
                                c = c0 + jo
                                P.op("vector", lambda e, jo=jo, c=c: e.tensor_tensor_scan(
                                    out=tG[:, jo, :], data0=tD[:, jo, :], data1=Bt[:, jo, :], initial=hst[:, c:c + 1], op0=ALU.mult, op1=ALU.add),
                                    reads=[f"tDj{jo}", f"tB{pb}j{jo}", "hst"], writes=[f"tGj{jo}"])
                            P.op("vector", lambda e: e.tensor_copy(out=hst[:, c0:c0 + 2], in_=tG[:, :, 511]), reads=["tGj0", "tGj1"], writes=["hst"])
                            P.op("vector", lambda e: e.tensor_tensor(out=tI[:], in0=tI[:], in1=Ht[:], op=ALU.mult), reads=[f"tH{pb}"], writes=["tI"])
                            P.op("vector", lambda e: e.tensor_tensor(out=yT[:, c0:c0 + 2, :], in0=tG[:], in1=tI[:], op=ALU.mult),
                                 reads=["tGj0", "tGj1", "tI"], writes=["yT"])

                        wmm(0)
                        ev_(0)
                        conv_(0)
                        gelupre_(0)
                        for b in range(10):
                            more = b + 1 < 10
                            if more:
                                wslots[b + 1] = load_block_w(b + 1)
                                wmm(b + 1)
                            gates(b, 0)
                            if more:
                                ev_(b + 1)
                            gates(b, 1)
                            if more:
                                conv_(b + 1)
                            st2(b)
                            if more:
                                gelupre_(b + 1)
                        for n in range(8):
                            ws = cnt["wo"] % 2
                            cnt["wo"] += 1
                            P.op("gpsimd", lambda e, ws=ws, n=n: e.dma_start(
                                out=wos[ws][:], in_=w_out[lj][:, n * 256:(n + 1) * 256].rearrange("(c p) d -> p c d", p=128)),
                                writes=[f"wos{ws}"], dsem=f"D_wo{ws}")
                            for t in range(NT):
                                bk = cnt["w"] % 4
                                cnt["w"] += 1
                                for k in range(RC):
                                    P.op("tensor", lambda e, bk=bk, ws=ws, k=k, t=t: e.matmul(
                                        psW[:, bk, 0:256], lhsT=yT[:, k, t * 128:(t + 1) * 128], rhs=wos[ws][:, k, :],
                                        start=(k == 0), stop=(k == RC - 1)),
                                        reads=[f"wos{ws}", "yT"], writes=[f"psWo{bk}", "psWx", "psWg"], sig=(k == RC - 1))
                                P.op("vector", lambda e, bk=bk, t=t, n=n: e.tensor_tensor(
                                    out=xacc[:, t, n * 256:(n + 1) * 256], in0=psW[:, bk, 0:256], in1=xacc[:, t, n * 256:(n + 1) * 256], op=ALU.add),
                                    reads=[f"psWo{bk}"], writes=[f"xacc{t}"])
                        for t in range(NT):
                            P.op("sync", lambda e, t=t, r0=r0: e.dma_start(out=dst[r0 + t * 128:r0 + (t + 1) * 128, :], in_=xacc[:, t, :]),
                                 reads=[f"xacc{t}"], writes=[f"dst{r0 + t * 128}"], dsem=f"D_xs{t}")
                P.barrier()

        def phase_att_qkv(lj, gi, src):
            TT, NT = 1024, 8
            with contextlib.ExitStack() as ps:
                ev = Env()
                norm_bufs(ev, ps, gi)
                xt = [sbt(ps, f"xt{i}", [128, D], F32) for i in range(2)]
                hT = sbt(ps, "hT", [128, DC, TT], BF16)
                wq = [sbt(ps, f"wq{i}", [128, DC, 128], BF16) for i in range(4)]
                wv = [sbt(ps, f"wv{i}", [128, DC, 512], BF16) for i in range(2)]
                qs = [sbt(ps, f"qs{i}", [128, TT], BF16) for i in range(4)]
                t1 = [sbt(ps, f"t1{i}", [128, 512], F32) for i in range(2)]
                t2 = [sbt(ps, f"t2{i}", [128, 512], F32) for i in range(2)]
                cos = sbt(ps, "cos", [128, S], F32)
                sin = sbt(ps, "sin", [128, S], F32)
                vs = sbt(ps, "vs", [128, NT, D], BF16)
                psQ = [pst(ps, f"psQ{i}", [128, 512]) for i in range(3)]
                psR = [pst(ps, f"psR{i}", [128, 512]) for i in range(2)]
                P.op("sync", lambda e: e.dma_start(out=cos[:], in_=rope[0]), writes=["cos"], dsem="D_cos")
                P.op("sync", lambda e: e.dma_start(out=sin[:], in_=rope[1]), writes=["sin"], dsem="D_sin")
                cnt = {"wq": 0, "wv": 0, "q": 0, "r": 0, "qs": 0, "t": 0}
                for p in range(NTOK // TT):
                    r0 = p * TT
                    sq = r0 // S
                    pos0 = r0 % S
                    for t in range(NT):
                        sl = t % 2
                        P.op("sync", lambda e, t=t, sl=sl, r0=r0: e.dma_start(out=xt[sl][:], in_=src[r0 + t * 128:r0 + (t + 1) * 128, :]),
                             writes=[f"xt{sl}"], dsem=f"D_x{sl}")
                        norm_tile(ev, xt[sl][:], f"xt{sl}", ev.g[:], hT, f"hT{t}", t * 128, t)
                    pendq = []
                    for hd in range(NH):
                        for qk in range(2):
                            ws = cnt["wq"] % 4
                            cnt["wq"] += 1
                            c0 = qk * D + hd * 128
                            P.op("gpsimd", lambda e, ws=ws, c0=c0: e.dma_start(
                                out=wq[ws][:], in_=w_qkv[lj][:, c0:c0 + 128].rearrange("(c p) f -> p c f", p=128)),
                                writes=[f"wq{ws}"], dsem=f"D_wq{ws}")
                            qi = cnt["qs"] % 4
                            cnt["qs"] += 1
                            for hf in range(2):
                                bk = cnt["q"] % 3
                                cnt["q"] += 1
                                for k in range(DC):
                                    P.op("tensor", lambda e, bk=bk, ws=ws, k=k, hf=hf: e.matmul(
                                        psQ[bk][:], lhsT=wq[ws][:, k, :], rhs=hT[:, k, hf * 512:(hf + 1) * 512],
                                        start=(k == 0), stop=(k == DC - 1)),
                                        reads=[f"wq{ws}"] + [f"hT{t}" for t in range(hf * 4, hf * 4 + 4)],
                                        writes=[f"psQ{bk}"], sig=(k == DC - 1))
                                qv = qs[qi][:, hf * 512:(hf + 1) * 512]
                                P.op("scalar", lambda e, bk=bk, qv=qv: e.copy(out=qv, in_=psQ[bk][:]),
                                     reads=[f"psQ{bk}"], writes=[f"qs{qi}h{hf}"])
                                def fin(bk=bk, qi=qi, hf=hf, qk=qk, hd=hd):
                                    rb = cnt["r"] % 2
                                    cnt["r"] += 1
                                    P.op("tensor", lambda e: e.matmul(
                                        psR[rb][:], lhsT=Rt, rhs=qs[qi][:, hf * 512:(hf + 1) * 512], start=True, stop=True),
                                        reads=[f"qs{qi}h{hf}", "cs"], writes=[f"psR{rb}"])
                                    ti = cnt["t"] % 2
                                    cnt["t"] += 1
                                    pc = pos0 + hf * 512
                                    P.op("vector", lambda e: e.tensor_tensor(
                                        out=t1[ti][:], in0=psQ[bk][:], in1=cos[:, pc:pc + 512], op=ALU.mult),
                                        reads=[f"psQ{bk}", "cos", f"qs{qi}h{hf}"], writes=[f"t1{ti}"])
                                    P.op("vector", lambda e: e.tensor_tensor(
                                        out=t2[ti][:], in0=psR[rb][:], in1=sin[:, pc:pc + 512], op=ALU.mult),
                                        reads=[f"psR{rb}", "sin"], writes=[f"t2{ti}"])
                                    P.op("vector", lambda e: e.tensor_tensor(
                                        out=qs[qi][:, hf * 512:(hf + 1) * 512], in0=t1[ti][:], in1=t2[ti][:], op=ALU.add),
                                        reads=[f"t1{ti}", f"t2{ti}"], writes=[f"qs{qi}h{hf}"])
                                    if hf == 1:
                                        dsc = qsc if qk == 0 else ksc
                                        P.op("sync", lambda e, sq=sq, pos0=pos0: e.dma_start(
                                            out=dsc[sq, hd, :, pos0:pos0 + TT], in_=qs[qi][:]),
                                            reads=[f"qs{qi}h0", f"qs{qi}h1"], writes=[f"qk{qk}_{sq}_{hd}_{pos0}"], dsem=f"D_qs{qi}")
                                pendq.append(fin)
                                if len(pendq) > 1:
                                    pendq.pop(0)()
                    while pendq:
                        pendq.pop(0)()
                    for n in range(4):
                        ws = cnt["wv"] % 2
                        cnt["wv"] += 1
                        c0 = 2 * D + n * 512
                        P.op("gpsimd", lambda e, ws=ws, c0=c0: e.dma_start(
                            out=wv[ws][:], in_=w_qkv[lj][:, c0:c0 + 512].rearrange("(c p) f -> p c f", p=128)),
                            writes=[f"wv{ws}"], dsem=f"D_wv{ws}")
                        for t in range(NT):
                            bk = cnt["q"] % 3
                            cnt["q"] += 1
                            for k in range(DC):
                                P.op("tensor", lambda e, bk=bk, ws=ws, k=k, t=t: e.matmul(
                                    psQ[bk][:], lhsT=hT[:, k, t * 128:(t + 1) * 128], rhs=wv[ws][:, k, :],
                                    start=(k == 0), stop=(k == DC - 1)),
                                    reads=[f"wv{ws}", f"hT{t}"], writes=[f"psQ{bk}"], sig=(k == DC - 1))
                            eng = "scalar" if t % 2 == 0 else "vector"
                            if eng == "scalar":
                                P.op("scalar", lambda e, bk=bk, t=t, n=n: e.copy(out=vs[:, t, n * 512:(n + 1) * 512], in_=psQ[bk][:]),
                                     reads=[f"psQ{bk}"], writes=[f"vs{t}"])
                            else:
                                P.op("vector", lambda e, bk=bk, t=t, n=n: e.tensor_copy(out=vs[:, t, n * 512:(n + 1) * 512], in_=psQ[bk][:]),
                                     reads=[f"psQ{bk}"], writes=[f"vs{t}"])
                    for t in range(NT):
                        P.op("sync", lambda e, t=t, r0=r0: e.dma_start(out=vsc[r0 + t * 128:r0 + (t + 1) * 128, :], in_=vs[:, t, :]),
                             reads=[f"vs{t}"], writes=[f"vsc{r0 + t * 128}"], dsem=f"D_vs{t}")
                P.barrier()

        def phase_att_core():
            SC = 128.0 ** -0.5
            with contextlib.ExitStack() as ps:
                qd = {}
                kd = {}
                vd = {}
                for par in range(2):
                    for d in (1, 4, 16):
                        qd[par, d] = sbt(ps, f"q{d}_{par}", [128, S], BF16)
                        kd[par, d] = sbt(ps, f"k{d}_{par}", [128, S], BF16)
                        vd[par, d] = sbt(ps, f"v{d}_{par}", [128, 16, 128], BF16)
                nacc = [sbt(ps, f"nacc{i}", [128, S], F32) for i in range(2)]
                dacc = [sbt(ps, f"dacc{i}", [128, S], F32) for i in range(2)]
                osb = [sbt(ps, f"osb{i}", [128, S], BF16) for i in range(2)]
                PT = [sbt(ps, f"PT{i}", [128, 512], BF16) for i in range(4)]
                mk4 = sbt(ps, "mk4", [128, 512], BF16)
                psS = [pst(ps, f"psS{i}", [128, 512]) for i in range(4)]
                psN = [pst(ps, f"psN{i}", [128, 512]) for i in range(2)]
                psD = [pst(ps, f"psD{i}", [128, 512]) for i in range(2)]
                P.op("gpsimd", lambda e: e.tensor_copy(out=mk4[:, 0:256], in_=mask2), reads=["cs"], writes=["mk4"])
                P.op("gpsimd", lambda e: e.tensor_copy(out=mk4[:, 256:512], in_=mask2), reads=["cs"], writes=["mk4"])
                cnt = {"s": 0, "g": 0}
                heads = [(sq, hd) for sq in range(nseq) for hd in range(NH)]

                def loads(hi):
                    sq, hd = heads[hi]
                    par = hi % 2
                    kq = f"_{par}"
                    P.op("sync", lambda e: e.dma_start(out=qd[par, 1][:], in_=qsc[sq, hd]), writes=["q1" + kq], dsem="D_q" + kq)
                    P.op("sync", lambda e: e.dma_start(out=kd[par, 1][:], in_=ksc[sq, hd]), writes=["k1" + kq], dsem="D_k" + kq)
                    for d in (1, 4, 16):
                        for r in range(d):
                            nb = 16 // d
                            srcv = vsc[sq * S:(sq + 1) * S, hd * 128:(hd + 1) * 128].rearrange("(n p r) c -> p r n c", p=128, r=d)[:, r]
                            P.op("sync", lambda e, d=d, r=r, nb=nb, srcv=srcv: e.dma_start(
                                out=vd[par, d][:, r * nb:(r + 1) * nb, :], in_=srcv),
                                writes=[f"v{d}r{r}" + kq], dsem=f"D_v{d}" + kq)

                def deint(hi):
                    par = hi % 2
                    kq = f"_{par}"
                    for d in (4, 16):
                        P.op("scalar", lambda e, d=d: e.copy(
                            out=qd[par, d][:].rearrange("p (r j) -> p r j", r=d), in_=qd[par, 1][:].rearrange("p (j r) -> p r j", r=d)),
                            reads=["q1" + kq], writes=[f"q{d}" + kq])
                        P.op("vector", lambda e, d=d: e.tensor_copy(
                            out=kd[par, d][:].rearrange("p (r j) -> p r j", r=d), in_=kd[par, 1][:].rearrange("p (j r) -> p r j", r=d)),
                            reads=["k1" + kq], writes=[f"k{d}" + kq])

                loads(0)
                deint(0)
                for hi, (sq, hd) in enumerate(heads):
                    par = hi % 2
                    kq = f"_{par}"
                    if hi + 1 < len(heads):
                        loads(hi + 1)
                    for bi, d in enumerate((1, 4, 16)):
                        L = S // d
                        nbl = L // 128
                        qa, ka, va = qd[par, d], kd[par, d], vd[par, d]
                        rk = [f"q{d}" + kq, f"k{d}" + kq]
                        vkeys = [f"v{d}r{r}" + kq for r in range(d)]

                        def smm(pp, qa=qa, ka=ka, nbl=nbl, rk=rk):
                            sb_ = cnt["s"] % 4
                            cnt["s"] += 1
                            hps = []
                            for w in range(2):
                                m = 2 * pp + w
                                hp = (m % nbl) != 0
                                hps.append(hp)
                                if hp:
                                    P.op("tensor", lambda e, m=m, w=w: e.matmul(
                                        psS[sb_][:, w * 256:w * 256 + 128], lhsT=ka[:, (m - 1) * 128:m * 128], rhs=qa[:, m * 128:(m + 1) * 128], start=True, stop=True),
                                        reads=rk, writes=[f"psS{sb_}"], sig=False)
                                P.op("tensor", lambda e, m=m, w=w: e.matmul(
                                    psS[sb_][:, w * 256 + 128:w * 256 + 256], lhsT=ka[:, m * 128:(m + 1) * 128], rhs=qa[:, m * 128:(m + 1) * 128], start=True, stop=True),
                                    reads=rk, writes=[f"psS{sb_}"], sig=(w == 1))
                            if hps[0] and hps[1]:
                                view = lambda t: t[:, 0:512]
                            elif hps[1]:
                                view = lambda t: t[:, 128:512]
                            else:
                                view = lambda t: t[:, :].rearrange("p (a b) -> p a b", a=2)[:, :, 128:256]
                            P.op("scalar", lambda e: e.activation(out=view(PT[sb_]), in_=view(psS[sb_]), func=AF.Exp, scale=SC),
                                 reads=[f"psS{sb_}"], writes=[f"PT{sb_}"])
                            meng = "gpsimd" if (cnt["s"] % 2) == 0 else "vector"
                            P.op(meng, lambda e: e.tensor_tensor(out=view(PT[sb_]), in0=view(PT[sb_]), in1=view(mk4), op=ALU.mult),
                                 reads=["mk4"], writes=[f"PT{sb_}"])
                            return hps, sb_

                        def pvmm(pp, hps, pi, gb, va=va, vkeys=vkeys):
                            for w in range(2):
                                m = 2 * pp + w
                                hp = hps[w]
                                co = (m % 4) * 128
                                last = (m % 4) == 3
                                for which, bank, key in ((0, psN[gb], f"psN{gb}"), (1, psD[gb], f"psD{gb}")):
                                    if hp:
                                        P.op("tensor", lambda e, bank=bank, which=which, m=m, co=co, w=w: e.matmul(
                                            bank[:, co:co + 128], lhsT=(va[:, m - 1, :] if which == 0 else ones), rhs=PT[pi][:, w * 256:w * 256 + 128],
                                            start=True, stop=False),
                                            reads=[f"PT{pi}", "cs"] + vkeys, writes=[key], sig=False)
                                    P.op("tensor", lambda e, bank=bank, which=which, m=m, co=co, hp=hp, w=w: e.matmul(
                                        bank[:, co:co + 128], lhsT=(va[:, m, :] if which == 0 else ones), rhs=PT[pi][:, w * 256 + 128:w * 256 + 256],
                                        start=(not hp), stop=True),
                                        reads=[f"PT{pi}", "cs"] + vkeys, writes=[key], sig=(which == 1 and w == 1))

                        pend = [smm(0), smm(1), smm(2)]
                        for pp in range(8):
                            cur = pend.pop(0)
                            if pp + 3 < 8:
                                pend.append(smm(pp + 3))
                            gb = cnt["g"] % 2
                            pvmm(pp, cur[0], cur[1], gb)
                            if pp % 2 == 1:
                                cnt["g"] += 1
                                m0 = 2 * pp - 2
                                for acc, bank, key, akey in ((nacc[par], psN[gb], f"psN{gb}", "nacc" + kq), (dacc[par], psD[gb], f"psD{gb}", "dacc" + kq)):
                                    av = acc[:].rearrange("p (j r) -> p r j", r=d)
                                    if d == 1:
                                        oview = av[:, 0, m0 * 128:m0 * 128 + 512]
                                        iview = bank[:]
                                    elif d == 4:
                                        oview = av[:, m0 // 4, :]
                                        iview = bank[:]
                                    else:
                                        oview = av[:, m0:m0 + 4, :]
                                        iview = bank[:].rearrange("p (a b) -> p a b", a=4)
                                    if bi == 0:
                                        P.op("scalar", lambda e, oview=oview, iview=iview: e.copy(out=oview, in_=iview),
                                             reads=[key], writes=[akey])
                                    else:
                                        P.op("vector", lambda e, oview=oview, iview=iview: e.tensor_tensor(out=oview, in0=iview, in1=oview, op=ALU.add),
                                             reads=[key], writes=[akey])
                        if bi == 1 and hi + 1 < len(heads):
                            deint(hi + 1)
                    P.op("scalar", lambda e, par=par: e.activation(out=dacc[par][:], in_=dacc[par][:], func=AF.Ln),
                         reads=[], writes=["dacc" + kq])
                    P.op("scalar", lambda e, par=par: e.activation(out=dacc[par][:], in_=dacc[par][:], func=AF.Exp, scale=-1.0),
                         reads=[], writes=["dacc" + kq])
                    P.op("vector", lambda e, par=par: e.tensor_tensor(out=osb[par][:], in0=nacc[par][:], in1=dacc[par][:], op=ALU.mult),
                         reads=["nacc" + kq, "dacc" + kq], writes=["osb" + kq])
                    P.op("sync", lambda e, par=par, sq=sq, hd=hd: e.dma_start(out=osc[sq, hd * 128:(hd + 1) * 128, :], in_=osb[par][:]),
                         reads=["osb" + kq], writes=[f"osc{sq}_{hd}"], dsem="D_os" + kq)
                P.barrier()

        def phase_att_out(lj, src, dst):
            TT, NT = 1024, 8
            with contextlib.ExitStack() as ps:
                xacc = sbt(ps, "xacc", [128, NT, D], F32)
                oT = sbt(ps, "oT", [128, NH, TT], BF16)
                wos = [sbt(ps, f"wos{i}", [128, NH, 512], BF16) for i in range(2)]
                psB = [pst(ps, f"psB{i}", [128, 512]) for i in range(4)]
                cnt = {"w": 0, "b": 0}
                for p in range(NTOK // TT):
                    r0 = p * TT
                    sq = r0 // S
                    pos0 = r0 % S
                    P.op("sync", lambda e, sq=sq, pos0=pos0: e.dma_start(
                        out=oT[:], in_=osc[sq][:, pos0:pos0 + TT].rearrange("(h p) t -> p h t", p=128)),
                        writes=["oT"], dsem="D_oT")
                    for t in range(NT):
                        P.op("sync", lambda e, t=t, r0=r0: e.dma_start(out=xacc[:, t, :], in_=src[r0 + t * 128:r0 + (t + 1) * 128, :]),
                             writes=[f"xacc{t}"], dsem=f"D_x{t}")
                    for n in range(4):
                        ws = cnt["w"] % 2
                        cnt["w"] += 1
                        P.op("gpsimd", lambda e, ws=ws, n=n: e.dma_start(
                            out=wos[ws][:], in_=w_o[lj][:, n * 512:(n + 1) * 512].rearrange("(c p) d -> p c d", p=128)),
                            writes=[f"wos{ws}"], dsem=f"D_wo{ws}")
                        for t in range(NT):
                            bk = cnt["b"] % 4
                            cnt["b"] += 1
                            for k in range(NH):
                                P.op("tensor", lambda e, bk=bk, ws=ws, k=k, t=t: e.matmul(
                                    psB[bk][:], lhsT=oT[:, k, t * 128:(t + 1) * 128], rhs=wos[ws][:, k, :],
                                    start=(k == 0), stop=(k == NH - 1)),
                                    reads=[f"wos{ws}", "oT"], writes=[f"psB{bk}"], sig=(k == NH - 1))
                            P.op("vector", lambda e, bk=bk, t=t, n=n: e.tensor_tensor(
                                out=xacc[:, t, n * 512:(n + 1) * 512], in0=psB[bk][:], in1=xacc[:, t, n * 512:(n + 1) * 512], op=ALU.add),
                                reads=[f"psB{bk}"], writes=[f"xacc{t}"])
                    for t in range(NT):
                        P.op("sync", lambda e, t=t, r0=r0: e.dma_start(out=dst[r0 + t * 128:r0 + (t + 1) * 128, :], in_=xacc[:, t, :]),
                             reads=[f"xacc{t}"], writes=[f"dst{r0 + t * 128}"], dsem=f"D_xs{t}")
                P.barrier()

        oneb_t = st.enter_context(nc.sbuf_tensor("oneb", [128, 1], F32))
        P.op("vector", lambda e: e.memset(oneb_t[:], 1.0), writes=["oneb"])
        Env.oneb = oneb_t
        P.barrier()
        cur = x_in
        nsub = len(plan)
        for si, (kind, idx, gi) in enumerate(plan):
            lastsub = si == nsub - 1
            dst = out if lastsub else xres
            if kind == "mlp":
                phase_mlp(idx, gi, cur, dst, final_norm and lastsub)
            elif kind == "lru":
                phase_lru(idx, gi, cur, dst)
            elif kind == "att":
                phase_att_qkv(idx, gi, cur)
                phase_att_core()
                phase_att_out(idx, cur, dst)
            elif kind == "attq":
                phase_att_qkv(idx, gi, cur)
            elif kind == "attqc":
                phase_att_qkv(idx, gi, cur)
                phase_att_core()
            cur = dst
        P.final_waits = {"sync": {s: v for s, v in P.cnt.items() if v > 0}}
        with nc.Block() as block:
            P.emit(block)
    return nc


def full_plan():
    plan = []
    for i in range(4):
        j = i // 2
        plan.append(("lru" if i % 2 == 0 else "att", j, i))
        plan.append(("mlp", i, 4 + i))
    return plan


def host_consts():
    bf = ml_dtypes.bfloat16
    cst = np.zeros((128, 640), np.float32)
    cst[:, 0:128] = np.eye(128)
    cst[:, 128:256] = 1.0
    j = np.arange(128)[:, None]
    i = np.arange(128)[None, :]
    cst[:, 256:384] = (j >= i)
    cst[:, 384:512] = (j <= i)
    for m in range(32):
        if m < 16:
            cst[m + 16, 512 + m] = -1.0
        else:
            cst[m - 16, 512 + m] = 1.0
    inv = 500000.0 ** (-np.arange(0, 32, 2, dtype=np.float32) / 32)
    ang = np.arange(S, dtype=np.float32)[None, :] * np.concatenate([inv, inv]).astype(np.float32)[:, None]
    rope = np.zeros((2, 128, S), np.float32)
    rope[0] = 1.0
    rope[0, 0:32] = np.cos(ang)
    rope[1, 0:32] = np.sin(ang)
    return cst.astype(bf), rope


def prep_shared(inp):
    f = lambda a: np.ascontiguousarray(np.asarray(a, dtype=np.float32))
    gains = np.concatenate([f(inp["mix_norm"]), f(inp["mlp_norm"]), f(inp["final_norm"])[None]], axis=0)
    gains = np.ascontiguousarray(np.broadcast_to(gains[:, None, :], (9, 128, D)))
    vecs = np.concatenate([f(inp["lru_conv_w"]), f(inp["lru_conv_b"])[:, None], f(inp["lru_b_a"]).reshape(2, 1, DR),
                           f(inp["lru_b_x"]).reshape(2, 1, DR), f(inp["lru_lambda"])[:, None]], axis=1)
    lruvec = np.ascontiguousarray(vecs.reshape(2, 8, RC, 128).transpose(0, 3, 1, 2))
    cst, rope = host_consts()
    sh = {"gains": gains, "lruvec": lruvec, "cst": cst, "rope": rope}
    for k in ("mlp_w1", "mlp_w2", "lru_w_in", "lru_w_a", "lru_w_x", "lru_w_out", "attn_w_qkv", "attn_w_o"):
        sh[k] = f(inp[k])
    return sh


def kernel(**inputs):
    n = 8
    x = np.asarray(inputs["x"], dtype=np.float32)
    B = x.shape[0]
    nseq = B // n
    sh = prep_shared(inputs)
    nc = build(nseq, full_plan(), True)
    in_maps = []
    for c in range(n):
        m = dict(sh)
        m["x"] = np.ascontiguousarray(x[c * nseq:(c + 1) * nseq].reshape(nseq * S, D))
        in_maps.append(m)
    res = run_bass_kernel_spmd(nc, in_maps, core_ids=list(range(n)))
    outs = [np.asarray(r["out"], dtype=np.float32).reshape(nseq, S, D) for r in res.results]
    return np.concatenate(outs, axis=0)
```

```python
import contextlib
import os
import numpy as np
import ml_dtypes
import concourse.bass as bass
import concourse.mybir as mybir
from concourse.bass_utils import run_bass_kernel_spmd

F32 = mybir.dt.float32
BF16 = mybir.dt.bfloat16
AF = mybir.ActivationFunctionType
ALU = mybir.AluOpType

D = 2048
DC = 16
S = 2048
FF = 8192
DR = 2560
RC = 20
NH = 16
EPS = 1e-6
ENGS = ("sync", "scalar", "gpsimd", "vector", "tensor")


class Prog:
    def __init__(self, nc, stack):
        self.nc = nc
        self.stack = stack
        self.ops = {e: [] for e in ENGS}
        self.sem = {}
        self.cnt = {}
        self.last_w = {}
        self.readers = {}
        self.final_waits = {}
        for e in ("scalar", "gpsimd", "vector", "tensor"):
            self.newsem("E_" + e)

    def newsem(self, name):
        if name not in self.sem:
            self.sem[name] = self.stack.enter_context(self.nc.semaphore(name))
            self.cnt[name] = 0

    def op(self, eng, fn, reads=(), writes=(), dsem=None, sig=True):
        waits = {}
        own = "E_" + eng

        def addw(tok):
            if tok is None:
                return
            s, v = tok
            if s == own and eng == "tensor":
                return
            if waits.get(s, 0) < v:
                waits[s] = v

        for k in reads:
            addw(self.last_w.get(k))
        for k in writes:
            addw(self.last_w.get(k))
            for r in self.readers.get(k, ()):
                addw(r)
        if dsem is not None:
            self.newsem(dsem)
            sname, n = dsem, 16
            self.cnt[sname] += n
            tok = (sname, self.cnt[sname])
        elif sig:
            sname, n = own, 1
            self.cnt[sname] += 1
            tok = (sname, self.cnt[sname])
        else:
            sname, n = None, 0
            tok = (own, self.cnt[own] + 1)
        self.ops[eng].append((fn, waits, sname, n))
        for k in reads:
            self.readers.setdefault(k, []).append(tok)
        for k in writes:
            self.last_w[k] = tok
            self.readers[k] = []
        return tok

    def barrier(self):
        allw = {s: v for s, v in self.cnt.items() if v > 0}
        for e in ENGS:
            self.ops[e].append((None, dict(allw), None, 0))
        self.last_w = {}
        self.readers = {}

    def emit(self, block):
        prog = self

        def mk(ename):
            def body(eng):
                seen = {}
                for fn, waits, sname, n in prog.ops[ename]:
                    for s, v in sorted(waits.items()):
                        if seen.get(s, 0) < v:
                            eng.wait_ge(prog.sem[s], v)
                            seen[s] = v
                    if fn is None:
                        continue
                    ins = fn(eng)
                    if sname is not None:
                        ins.then_inc(prog.sem[sname], n)
                for s, v in sorted(prog.final_waits.get(ename, {}).items()):
                    if seen.get(s, 0) < v:
                        eng.wait_ge(prog.sem[s], v)
            return body

        for ename in ENGS:
            if prog.ops[ename] or prog.final_waits.get(ename):
                getattr(block, ename)(mk(ename))


class Env:
    pass


def build(nseq, plan, final_norm=True):
    NTOK = nseq * S
    nc = bass.Bass("TRN2", target_bir_lowering=False)
    dt_in = lambda name, shape, dt=F32: nc.dram_tensor(name, list(shape), dt, kind="ExternalInput").ap()
    x_in = dt_in("x", [NTOK, D])
    gains = dt_in("gains", [9, 128, D])
    w1 = dt_in("mlp_w1", [4, D, FF])
    w2 = dt_in("mlp_w2", [4, FF, D])
    w_in = dt_in("lru_w_in", [2, D, 2 * DR])
    lruvec = dt_in("lruvec", [2, 128, 8, RC])
    w_a = dt_in("lru_w_a", [2, 10, 256, 256])
    w_x = dt_in("lru_w_x", [2, 10, 256, 256])
    w_out = dt_in("lru_w_out", [2, DR, D])
    w_qkv = dt_in("attn_w_qkv", [2, D, 3 * D])
    w_o = dt_in("attn_w_o", [2, D, D])
    cst = dt_in("cst", [128, 640], BF16)
    rope = dt_in("rope", [2, 128, S])
    out = nc.dram_tensor("out", [NTOK, D], F32, kind="ExternalOutput").ap()
    xres = nc.dram_tensor("xres", [NTOK, D], F32).ap()
    qsc = nc.dram_tensor("qsc", [nseq, NH, 128, S], BF16).ap()
    ksc = nc.dram_tensor("ksc", [nseq, NH, 128, S], BF16).ap()
    vsc = nc.dram_tensor("vsc", [NTOK, D], BF16).ap()
    osc = nc.dram_tensor("osc", [nseq, D, S], BF16).ap()

    with contextlib.ExitStack() as st:
        P = Prog(nc, st)
        cs = st.enter_context(nc.sbuf_tensor("cs", [128, 640], BF16))
        P.op("sync", lambda e: e.dma_start(out=cs[:], in_=cst), writes=["cs"], dsem="D_cs")
        ident = cs[:, 0:128]
        ones = cs[:, 128:256]
        mask2 = cs[:, 256:512]
        Rt = cs[:, 512:640]

        uid = [0]

        def sbt(stack, name, shape, dt):
            uid[0] += 1
            return stack.enter_context(nc.sbuf_tensor(f"{name}_u{uid[0]}", list(shape), dt))

        def pst(stack, name, shape, dt=F32):
            uid[0] += 1
            return stack.enter_context(nc.psum_tensor(f"{name}_u{uid[0]}", list(shape), dt))

        def norm_tile(ev, xap, xkey, gap, hT, hkey, col0, i, res_scale=None):
            norm_pre(ev, xap, xkey, gap, i)
            norm_T(ev, hT, hkey, col0, i)

        def norm_pre(ev, xap, xkey, gap, i):
            sl = i % 2
            ss = ev.ss[:, sl:sl + 1]
            hb = ev.hb[sl]
            P.op("scalar", lambda e: e.activation(out=hb[:], in_=xap, func=AF.Square, accum_out=ss),
                 reads=[xkey], writes=[f"hb{sl}", f"ss{sl}"])
            P.op("scalar", lambda e: e.activation(out=ss, in_=ss, func=AF.Sqrt, scale=1.0 / D, bias=ev.epsb[:, 0:1]),
                 reads=[f"ss{sl}"], writes=[f"ss{sl}"])
            P.op("vector", lambda e: e.reciprocal(out=ss, in_=ss), reads=[f"ss{sl}"], writes=[f"ss{sl}"])
            P.op("vector", lambda e: e.scalar_tensor_tensor(out=hb[:], in0=xap, scalar=ss, in1=gap, op0=ALU.mult, op1=ALU.mult),
                 reads=[xkey, f"ss{sl}", "g"], writes=[f"hb{sl}"])

        def norm_T(ev, hT, hkey, col0, i):
            sl = i % 2
            hb = ev.hb[sl]
            if getattr(ev, "psTf", None) is not None:
                for q in range(4):
                    bank, bkey = ev.psTf[q]
                    for c in range(4):
                        cc = q * 4 + c
                        P.op("tensor", lambda e, bank=bank, c=c, cc=cc: e.matmul(
                            bank[:, c * 128:(c + 1) * 128], lhsT=hb[:, cc * 128:(cc + 1) * 128], rhs=ident, start=True, stop=True),
                            reads=[f"hb{sl}", "cs"], writes=[bkey], sig=(c == 3))
                    if q % 2 == 0:
                        P.op("scalar", lambda e, bank=bank, q=q: e.copy(
                            out=hT[:, q * 4:(q + 1) * 4, col0:col0 + 128], in_=bank[:].rearrange("p (a b) -> p a b", a=4)),
                            reads=[bkey], writes=[hkey])
                    else:
                        P.op("vector", lambda e, bank=bank, q=q: e.tensor_copy(
                            out=hT[:, q * 4:(q + 1) * 4, col0:col0 + 128], in_=bank[:].rearrange("p (a b) -> p a b", a=4)),
                            reads=[bkey], writes=[hkey])
                return
            for hf in range(2):
                pT = ev.psT[hf]
                for c in range(8):
                    cc = hf * 8 + c
                    P.op("tensor", lambda e, pT=pT, c=c, cc=cc: e.transpose(out=pT[:, c, :], in_=hb[:, cc * 128:(cc + 1) * 128], identity=ident),
                         reads=[f"hb{sl}", "cs"], writes=[f"psT{hf}"], sig=(c == 7))
                eng = "scalar" if hf == 0 else "vector"
                if eng == "scalar":
                    P.op("scalar", lambda e, pT=pT, hf=hf: e.copy(out=hT[:, hf * 8:(hf + 1) * 8, col0:col0 + 128], in_=pT[:]),
                         reads=[f"psT{hf}"], writes=[hkey])
                else:
                    P.op("vector", lambda e, pT=pT, hf=hf: e.tensor_copy(out=hT[:, hf * 8:(hf + 1) * 8, col0:col0 + 128], in_=pT[:]),
                         reads=[f"psT{hf}"], writes=[hkey])

        def norm_bufs(ev, ps, gi, own_psT=True):
            ev.hb = [sbt(ps, f"hb{i}", [128, D], BF16) for i in range(2)]
            ev.ss = sbt(ps, "ss", [128, 2], F32)
            ev.epsb = sbt(ps, "epsb", [128, 1], F32)
            ev.g = sbt(ps, "g", [128, D], F32)
            ev.psTf = None
            if own_psT:
                ev.psT = [pst(ps, f"psT{i}", [128, 8, 128], BF16) for i in range(2)]
            P.op("vector", lambda e: e.memset(ev.epsb[:], EPS), writes=["epsb"])
            P.op("sync", lambda e: e.dma_start(out=ev.g[:], in_=gains[gi]), writes=["g"], dsem="D_g")

        def phase_mlp(li, gi, src, dst, fin):
            TT, NT = 1024, 8
            with contextlib.ExitStack() as ps:
                ev = Env()
                norm_bufs(ev, ps, gi)
                xacc = sbt(ps, "xacc", [128, NT, D], F32)
                hT = sbt(ps, "hT", [128, DC, TT], BF16)
                uT = [sbt(ps, f"uT{i}", [128, 8, TT], BF16) for i in range(2)]
                w1s = [sbt(ps, f"w1s{i}", [128, DC, 256], BF16) for i in range(2)]
                w2s = [sbt(ps, f"w2s{i}", [128, 8, 512], BF16) for i in range(4)]
                rt = [sbt(ps, f"rt{i}", [128, 512], F32) for i in range(2)]
                psA = [pst(ps, f"psA{i}", [128, 512]) for i in range(2)]
                psB = [pst(ps, f"psB{i}", [128, 512]) for i in range(4)]
                ssfin = sbt(ps, "ssfin", [128, 2], F32)
                if fin:
                    gf = sbt(ps, "gf", [128, D], F32)
                    P.op("sync", lambda e: e.dma_start(out=gf[:], in_=gains[8]), writes=["gf"], dsem="D_gf")
                cnt = {"w1": 0, "w2": 0, "a": 0, "b": 0, "r": 0}
                NPASS = NTOK // TT

                def load_tile(t, r0):
                    P.op("sync", lambda e, t=t, r0=r0: e.dma_start(out=xacc[:, t, :], in_=src[r0 + t * 128:r0 + (t + 1) * 128, :]),
                         writes=[f"xacc{t}"], dsem=f"D_x{t}")

                def finish_tile(t, r0):
                    if fin:
                        sl = t % 2
                        ss = ev.ss[:, sl:sl + 1]
                        ssf = ssfin[:, sl:sl + 1]
                        P.op("scalar", lambda e, t=t, ssf=ssf: e.activation(
                            out=uT[0][:, 0:2, :], in_=xacc[:, t, :].rearrange("p (a b) -> p a b", a=2), func=AF.Square, accum_out=ssf),
                            reads=[f"xacc{t}"], writes=["uT0h0", "uT0h1", f"ssf{sl}"])
                        P.op("scalar", lambda e, ssf=ssf: e.activation(out=ssf, in_=ssf, func=AF.Sqrt, scale=1.0 / D, bias=ev.epsb[:, 0:1]),
                             reads=[f"ssf{sl}"], writes=[f"ssf{sl}"])
                        P.op("vector", lambda e, ssf=ssf: e.reciprocal(out=ssf, in_=ssf), reads=[f"ssf{sl}"], writes=[f"ssf{sl}"])
                        P.op("vector", lambda e, t=t, ssf=ssf: e.scalar_tensor_tensor(
                            out=xacc[:, t, :], in0=xacc[:, t, :], scalar=ssf, in1=gf[:], op0=ALU.mult, op1=ALU.mult),
                            reads=[f"ssf{sl}", "gf"], writes=[f"xacc{t}"])
                    P.op("sync", lambda e, t=t, r0=r0: e.dma_start(out=dst[r0 + t * 128:r0 + (t + 1) * 128, :], in_=xacc[:, t, :]),
                         reads=[f"xacc{t}"], writes=[f"dst{r0 + t * 128}"], dsem=f"D_xs{t}")

                for p in range(NPASS):
                    r0 = p * TT
                    if p == 0:
                        for t in range(NT):
                            load_tile(t, r0)
                        for t in range(NT):
                            norm_tile(ev, xacc[:, t, :], f"xacc{t}", ev.g[:], hT, f"hT{t}", t * 128, t)

                    def stepA(gidx):
                        sl = gidx % 2
                        for pc in range(4):
                            ws = cnt["w1"] % 2
                            cnt["w1"] += 1
                            f0 = (gidx * 8 + pc * 2) * 128
                            P.op("gpsimd", lambda e, ws=ws, f0=f0: e.dma_start(
                                out=w1s[ws][:], in_=w1[li][:, f0:f0 + 256].rearrange("(c p) f -> p c f", p=128)),
                                writes=[f"w1s{ws}"], dsem=f"D_w1{ws}")
                            for j in range(2):
                                fc = pc * 2 + j
                                for hf in range(2):
                                    bk = cnt["a"] % 2
                                    cnt["a"] += 1
                                    for k in range(DC):
                                        P.op("tensor", lambda e, bk=bk, ws=ws, k=k, j=j, hf=hf: e.matmul(
                                            psA[bk][:], lhsT=w1s[ws][:, k, j * 128:(j + 1) * 128], rhs=hT[:, k, hf * 512:(hf + 1) * 512],
                                            start=(k == 0), stop=(k == DC - 1)),
                                            reads=[f"w1s{ws}"] + [f"hT{t}" for t in range(hf * 4, hf * 4 + 4)],
                                            writes=[f"psA{bk}"], sig=(k == DC - 1))
                                    ri = cnt["r"] % 2
                                    cnt["r"] += 1
                                    P.op("scalar", lambda e, bk=bk, ri=ri: e.activation(out=rt[ri][:], in_=psA[bk][:], func=AF.Relu),
                                         reads=[f"psA{bk}"], writes=[f"rt{ri}"])
                                    P.op("vector", lambda e, ri=ri, sl=sl, fc=fc, hf=hf: e.tensor_tensor(
                                        out=uT[sl][:, fc, hf * 512:(hf + 1) * 512], in0=rt[ri][:], in1=rt[ri][:], op=ALU.mult),
                                        reads=[f"rt{ri}"], writes=[f"uT{sl}h{hf}"])

                    def stepB(gidx):
                        sl = gidx % 2
                        for n in range(4):
                            ws = cnt["w2"] % 4
                            cnt["w2"] += 1
                            P.op("gpsimd", lambda e, ws=ws, n=n: e.dma_start(
                                out=w2s[ws][:], in_=w2[li][gidx * 1024:(gidx + 1) * 1024, n * 512:(n + 1) * 512].rearrange("(c p) d -> p c d", p=128)),
                                writes=[f"w2s{ws}"], dsem=f"D_w2{ws}")
                            for t in range(NT):
                                bk = cnt["b"] % 4
                                cnt["b"] += 1
                                for k in range(8):
                                    P.op("tensor", lambda e, bk=bk, ws=ws, k=k, t=t: e.matmul(
                                        psB[bk][:], lhsT=uT[sl][:, k, t * 128:(t + 1) * 128], rhs=w2s[ws][:, k, :],
                                        start=(k == 0), stop=(k == 7)),
                                        reads=[f"w2s{ws}", f"uT{sl}h{t // 4}"], writes=[f"psB{bk}"], sig=(k == 7))
                                P.op("vector", lambda e, bk=bk, t=t, n=n: e.tensor_tensor(
                                    out=xacc[:, t, n * 512:(n + 1) * 512], in0=psB[bk][:], in1=xacc[:, t, n * 512:(n + 1) * 512], op=ALU.add),
                                    reads=[f"psB{bk}"], writes=[f"xacc{t}"])

                    def stepB_last(gidx, prefetch):
                        sl = gidx % 2
                        wsl = []
                        for n in range(4):
                            ws = cnt["w2"] % 4
                            cnt["w2"] += 1
                            P.op("gpsimd", lambda e, ws=ws, n=n: e.dma_start(
                                out=w2s[ws][:], in_=w2[li][gidx * 1024:(gidx + 1) * 1024, n * 512:(n + 1) * 512].rearrange("(c p) d -> p c d", p=128)),
                                writes=[f"w2s{ws}"], dsem=f"D_w2{ws}")
                            wsl.append(ws)
                        for t in range(NT):
                            for n in range(4):
                                ws = wsl[n]
                                bk = cnt["b"] % 4
                                cnt["b"] += 1
                                for k in range(8):
                                    P.op("tensor", lambda e, bk=bk, ws=ws, k=k, t=t: e.matmul(
                                        psB[bk][:], lhsT=uT[sl][:, k, t * 128:(t + 1) * 128], rhs=w2s[ws][:, k, :],
                                        start=(k == 0), stop=(k == 7)),
                                        reads=[f"w2s{ws}", f"uT{sl}h{t // 4}"], writes=[f"psB{bk}"], sig=(k == 7))
                                P.op("vector", lambda e, bk=bk, t=t, n=n: e.tensor_tensor(
                                    out=xacc[:, t, n * 512:(n + 1) * 512], in0=psB[bk][:], in1=xacc[:, t, n * 512:(n + 1) * 512], op=ALU.add),
                                    reads=[f"psB{bk}"], writes=[f"xacc{t}"])
                            finish_tile(t, r0)
                            if prefetch:
                                load_tile(t, r0 + TT)
                                if t >= 1:
                                    norm_pre(ev, xacc[:, t - 1, :], f"xacc{t - 1}", ev.g[:], t - 1)
                                if t >= 2:
                                    norm_T(ev, hT, f"hT{t - 2}", (t - 2) * 128, t - 2)
                        if prefetch:
                            norm_pre(ev, xacc[:, NT - 1, :], f"xacc{NT - 1}", ev.g[:], NT - 1)
                            norm_T(ev, hT, f"hT{NT - 2}", (NT - 2) * 128, NT - 2)
                            norm_T(ev, hT, f"hT{NT - 1}", (NT - 1) * 128, NT - 1)

                    stepA(0)
                    for gidx in range(8):
                        if gidx + 1 < 8:
                            stepA(gidx + 1)
                        if gidx < 7:
                            stepB(gidx)
                        else:
                            stepB_last(gidx, p + 1 < NPASS)
                P.barrier()

        def phase_lru(lj, gi, src, dst):
            TT, NT = 512, 4
            with contextlib.ExitStack() as ps:
                ev = Env()
                norm_bufs(ev, ps, gi, own_psT=False)
                xacc = sbt(ps, "xacc", [128, NT, D], F32)
                hT = sbt(ps, "hT", [128, DC, TT], BF16)
                yT = sbt(ps, "yT", [128, RC, TT], BF16)
                wis = [sbt(ps, f"wis{i}", [128, DC, 256], BF16) for i in range(4)]
                wg = [sbt(ps, f"wg{i}", [128, 10, 2, 256], BF16) for i in range(2)]
                wos = [sbt(ps, f"wos{i}", [128, RC, 256], BF16) for i in range(2)]
                vec = sbt(ps, "vec", [128, 8, RC], F32)
                nsp = sbt(ps, "nsp", [128, RC], F32)
                nsp2 = sbt(ps, "nsp2", [128, RC], F32)
                hst = sbt(ps, "hst", [128, RC], F32)
                halo = sbt(ps, "halo", [128, RC, 4], F32)
                tA = sbt(ps, "tA", [128, 2, 516], F32)
                tB = [sbt(ps, f"tB{i}", [128, 2, 512], F32) for i in range(2)]
                tH = [sbt(ps, f"tH{i}", [128, 2, 512], F32) for i in range(2)]
                tD = sbt(ps, "tD", [128, 2, 512], F32)
                tE = sbt(ps, "tE", [128, 2, 512], F32)
                tF = sbt(ps, "tF", [128, 2, 512], F32)
                tG = sbt(ps, "tG", [128, 2, 512], F32)
                tI = sbt(ps, "tI", [128, 2, 512], F32)
                tC = sbt(ps, "tC", [128, 2, 512], BF16)
                psW = pst(ps, "psW", [128, 4, 512])
                psG = [pst(ps, f"psG{i}", [128, 512]) for i in range(4)]
                ev.psTf = [(psG[i], f"psG{i}") for i in range(4)]
                P.op("sync", lambda e: e.dma_start(out=vec[:], in_=lruvec[lj]), writes=["vec"], dsem="D_vec")
                for gt, wsrc in enumerate((w_a, w_x)):
                    for b in range(10):
                        P.op("gpsimd", lambda e, gt=gt, b=b, wsrc=wsrc: e.dma_start(
                            out=wg[gt][:, b, :, :], in_=wsrc[lj, b].rearrange("(c p) n -> p c n", p=128)),
                            writes=[f"wg{gt}"], dsem=f"D_wg{gt}")
                P.op("scalar", lambda e: e.activation(out=nsp[:], in_=vec[:, 7, :], func=AF.Exp, scale=-1.0), reads=["vec"], writes=["nsp"])
                P.op("scalar", lambda e: e.activation(out=nsp[:], in_=nsp[:], func=AF.Ln, bias=1.0), reads=["nsp"], writes=["nsp"])
                P.op("vector", lambda e: e.tensor_scalar(out=nsp2[:], in0=nsp[:], scalar1=-16.0, scalar2=None, op0=ALU.mult), reads=["nsp"], writes=["nsp2"])
                P.op("vector", lambda e: e.tensor_scalar(out=nsp[:], in0=nsp[:], scalar1=-8.0, scalar2=None, op0=ALU.mult), reads=["nsp", "nsp2"], writes=["nsp"])
                cnt = {"wi": 0, "g": 0, "wo": 0, "w": 0}

                def load_block_w(b):
                    res = []
                    for part in range(2):
                        ws = cnt["wi"] % 4
                        cnt["wi"] += 1
                        c0 = part * DR + b * 256
                        P.op("gpsimd", lambda e, ws=ws, c0=c0: e.dma_start(
                            out=wis[ws][:], in_=w_in[lj][:, c0:c0 + 256].rearrange("(c p) f -> p c f", p=128)),
                            writes=[f"wis{ws}"], dsem=f"D_wi{ws}")
                        res.append(ws)
                    return res

                for sq in range(nseq):
                    P.op("vector", lambda e: e.memset(hst[:], 0.0), writes=["hst"])
                    P.op("vector", lambda e: e.memset(halo[:], 0.0), writes=["halo"])
                    for ch in range(S // TT):
                        r0 = sq * S + ch * TT
                        for t in range(NT):
                            P.op("sync", lambda e, t=t, r0=r0: e.dma_start(out=xacc[:, t, :], in_=src[r0 + t * 128:r0 + (t + 1) * 128, :]),
                                 writes=[f"xacc{t}"], dsem=f"D_x{t}")
                        wslots = {0: load_block_w(0)}
                        for t in range(NT):
                            norm_tile(ev, xacc[:, t, :], f"xacc{t}", ev.g[:], hT, "hT", t * 128, t)

                        def wmm(b):
                            slots = wslots[b]
                            for part in range(2):
                                ws = slots[part]
                                for jj in range(2):
                                    bk = part * 2 + jj
                                    for k in range(DC):
                                        P.op("tensor", lambda e, bk=bk, ws=ws, k=k, jj=jj: e.matmul(
                                            psW[:, bk, :], lhsT=wis[ws][:, k, jj * 128:(jj + 1) * 128], rhs=hT[:, k, :],
                                            start=(k == 0), stop=(k == DC - 1)),
                                            reads=[f"wis{ws}", "hT"], writes=["psWx" if part == 0 else "psWg"], sig=(k == DC - 1))

                        def ev_(b):
                            pb = b % 2
                            c0 = 2 * b
                            Ht = tH[pb]
                            P.op("vector", lambda e: e.tensor_copy(out=tA[:, :, 0:3], in_=halo[:, c0:c0 + 2, 0:3]), reads=["halo"], writes=["tA"])
                            P.op("scalar", lambda e: e.copy(out=tA[:, :, 3:515], in_=psW[:, 0:2, :]), reads=["psWx"], writes=["tA"])
                            P.op("scalar", lambda e: e.copy(out=Ht[:], in_=psW[:, 2:4, :]), reads=["psWg"], writes=[f"tH{pb}"])
                            P.op("vector", lambda e: e.tensor_copy(out=halo[:, c0:c0 + 2, 0:3], in_=tA[:, :, 512:515]), reads=["tA"], writes=["halo"])

                        def conv_(b):
                            pb = b % 2
                            c0 = 2 * b
                            Bt = tB[pb]
                            for jj in range(2):
                                c = c0 + jj
                                P.op("vector", lambda e, jj=jj, c=c: e.tensor_scalar(
                                    out=Bt[:, jj, :], in0=tA[:, jj, 3:515], scalar1=vec[:, 3, c:c + 1], scalar2=vec[:, 4, c:c + 1],
                                    op0=ALU.mult, op1=ALU.add),
                                    reads=["tA", "vec"], writes=[f"tB{pb}j{jj}"])
                            for k in range(3):
                                for jj in range(2):
                                    c = c0 + jj
                                    P.op("vector", lambda e, jj=jj, c=c, k=k: e.scalar_tensor_tensor(
                                        out=Bt[:, jj, :], in0=tA[:, jj, k:k + 512], scalar=vec[:, k, c:c + 1], in1=Bt[:, jj, :], op0=ALU.mult, op1=ALU.add),
                                        reads=["tA", "vec"], writes=[f"tB{pb}j{jj}"])
                            P.op("vector", lambda e: e.tensor_copy(out=tC[:], in_=Bt[:]), reads=[f"tB{pb}j0", f"tB{pb}j1"], writes=["tC"])

                        def gelupre_(b):
                            pb = b % 2
                            Ht = tH[pb]
                            P.op("vector", lambda e: e.tensor_tensor(out=tI[:], in0=Ht[:], in1=Ht[:], op=ALU.mult), reads=[f"tH{pb}"], writes=["tI"])
                            P.op("vector", lambda e: e.tensor_scalar(out=tI[:], in0=tI[:], scalar1=0.044715, scalar2=1.0, op0=ALU.mult, op1=ALU.add),
                                 reads=["tI"], writes=["tI"])
                            P.op("vector", lambda e: e.tensor_tensor(out=tI[:], in0=tI[:], in1=Ht[:], op=ALU.mult), reads=[f"tH{pb}"], writes=["tI"])

                        def gates(b, gt):
                            if True:
                                for jo in range(2):
                                    bk = cnt["g"] % 4
                                    cnt["g"] += 1
                                    c = 2 * b + jo
                                    for jj in range(2):
                                        P.op("tensor", lambda e, bk=bk, gt=gt, jo=jo, jj=jj: e.matmul(
                                            psG[bk][:], lhsT=wg[gt][:, b, jj, jo * 128:(jo + 1) * 128], rhs=tC[:, jj, :],
                                            start=(jj == 0), stop=(jj == 1)),
                                            reads=[f"wg{gt}", "tC"], writes=[f"psG{bk}"], sig=(jj == 1))
                                    dstt = tD if gt == 0 else tE
                                    nm = "D" if gt == 0 else "E"
                                    P.op("scalar", lambda e, bk=bk, dstt=dstt, gt=gt, c=c, jo=jo: e.activation(
                                        out=dstt[:, jo, :], in_=psG[bk][:], func=AF.Sigmoid, bias=vec[:, 5 + gt, c:c + 1]),
                                        reads=[f"psG{bk}", "vec"], writes=[f"t{nm}j{jo}"])

                        def st2(b):
                            pb = b % 2
                            c0 = 2 * b
                            Bt, Ht = tB[pb], tH[pb]
                            for jo in range(2):
                                c = c0 + jo
                                P.op("scalar", lambda e, jo=jo, c=c: e.activation(out=tF[:, jo, :], in_=tD[:, jo, :], func=AF.Exp, scale=nsp2[:, c:c + 1]),
                                     reads=[f"tDj{jo}", "nsp2"], writes=[f"tFj{jo}"])
                            for jo in range(2):
                                c = c0 + jo
                                P.op("scalar", lambda e, jo=jo, c=c: e.activation(out=tD[:, jo, :], in_=tD[:, jo, :], func=AF.Exp, scale=nsp[:, c:c + 1]),
                                     reads=[f"tDj{jo}", "nsp"], writes=[f"tDj{jo}"])
                            P.op("vector", lambda e: e.tensor_scalar(out=tF[:], in0=tF[:], scalar1=1.0, scalar2=None, op0=ALU.min),
                                 reads=["tFj0", "tFj1"], writes=["tFj0", "tFj1"])
                            P.op("scalar", lambda e: e.activation(out=tF[:], in_=tF[:], func=AF.Sqrt, scale=-1.0, bias=1.0),
                                 reads=["tFj0", "tFj1"], writes=["tFj0", "tFj1"])
                            P.op("scalar", lambda e: e.activation(out=tI[:], in_=tI[:], func=AF.Sigmoid, scale=1.5957691), reads=["tI"], writes=["tI"])
                            P.op("vector", lambda e: e.tensor_tensor(out=Bt[:], in0=Bt[:], in1=tE[:], op=ALU.mult),
                                 reads=["tEj0", "tEj1"], writes=[f"tB{pb}j0", f"tB{pb}j1"])
                            P.op("vector", lambda e: e.tensor_tensor(out=Bt[:], in0=Bt[:], in1=tF[:], op=ALU.mult),
                                 reads=["tFj0", "tFj1"], writes=[f"tB{pb}j0", f"tB{pb}j1"])
                            for jo in range(2):
                                c = c0 + jo
                                P.op("vector", lambda e, jo=jo, c=c: e.tensor_tensor_scan(
                                    out=tG[:, jo, :], data0=tD[:, jo, :], data1=Bt[:, jo, :], initial=hst[:, c:c + 1], op0=ALU.mult, op1=ALU.add),
                                    reads=[f"tDj{jo}", f"tB{pb}j{jo}", "hst"], writes=[f"tGj{jo}"])
                            P.op("vector", lambda e: e.tensor_copy(out=hst[:, c0:c0 + 2], in_=tG[:, :, 511]), reads=["tGj0", "tGj1"], writes=["hst"])
                            P.op("vector", lambda e: e.tensor_tensor(out=tI[:], in0=tI[:], in1=Ht[:], op=ALU.mult), reads=[f"tH{pb}"], writes=["tI"])
                            P.op("vector", lambda e: e.tensor_tensor(out=yT[:, c0:c0 + 2, :], in0=tG[:], in1=tI[:], op=ALU.mult),
                                 reads=["tGj0", "tGj1", "tI"], writes=["yT"])

                        wmm(0)
                        ev_(0)
                        conv_(0)
                        gelupre_(0)
                        for b in range(10):
                            more = b + 1 < 10
                            if more:
                                wslots[b + 1] = load_block_w(b + 1)
                                wmm(b + 1)
                            gates(b, 0)
                            if more:
                                ev_(b + 1)
                            gates(b, 1)
                            if more:
                                conv_(b + 1)
                            st2(b)
                            if more:
                                gelupre_(b + 1)
                        for n in range(8):
                            ws = cnt["wo"] % 2
                            cnt["wo"] += 1
                            P.op("gpsimd", lambda e, ws=ws, n=n: e.dma_start(
                                out=wos[ws][:], in_=w_out[lj][:, n * 256:(n + 1) * 256].rearrange("(c p) d -> p c d", p=128)),
                                writes=[f"wos{ws}"], dsem=f"D_wo{ws}")
                            for t in range(NT):
                                bk = cnt["w"] % 4
                                cnt["w"] += 1
                                for k in range(RC):
                                    P.op("tensor", lambda e, bk=bk, ws=ws, k=k, t=t: e.matmul(
                                        psW[:, bk, 0:256], lhsT=yT[:, k, t * 128:(t + 1) * 128], rhs=wos[ws][:, k, :],
                                        start=(k == 0), stop=(k == RC - 1)),
                                        reads=[f"wos{ws}", "yT"], writes=[f"psWo{bk}", "psWx", "psWg"], sig=(k == RC - 1))
                                P.op("vector", lambda e, bk=bk, t=t, n=n: e.tensor_tensor(
                                    out=xacc[:, t, n * 256:(n + 1) * 256], in0=psW[:, bk, 0:256], in1=xacc[:, t, n * 256:(n + 1) * 256], op=ALU.add),
                                    reads=[f"psWo{bk}"], writes=[f"xacc{t}"])
                        for t in range(NT):
                            P.op("sync", lambda e, t=t, r0=r0: e.dma_start(out=dst[r0 + t * 128:r0 + (t + 1) * 128, :], in_=xacc[:, t, :]),
                                 reads=[f"xacc{t}"], writes=[f"dst{r0 + t * 128}"], dsem=f"D_xs{t}")
                P.barrier()

        def phase_att_qkv(lj, gi, src):
            TT, NT = 1024, 8
            with contextlib.ExitStack() as ps:
                ev = Env()
                norm_bufs(ev, ps, gi)
                xt = [sbt(ps, f"xt{i}", [128, D], F32) for i in range(2)]
                hT = sbt(ps, "hT", [128, DC, TT], BF16)
                wq = [sbt(ps, f"wq{i}", [128, DC, 128], BF16) for i in range(4)]
                wv = [sbt(ps, f"wv{i}", [128, DC, 512], BF16) for i in range(2)]
                qs = [sbt(ps, f"qs{i}", [128, TT], BF16) for i in range(4)]
                t1 = [sbt(ps, f"t1{i}", [128, 512], F32) for i in range(2)]
                t2 = [sbt(ps, f"t2{i}", [128, 512], F32) for i in range(2)]
                cos = sbt(ps, "cos", [128, S], F32)
                sin = sbt(ps, "sin", [128, S], F32)
                vs = sbt(ps, "vs", [128, NT, D], BF16)
                psQ = [pst(ps, f"psQ{i}", [128, 512]) for i in range(3)]
                psR = [pst(ps, f"psR{i}", [128, 512]) for i in range(2)]
                P.op("sync", lambda e: e.dma_start(out=cos[:], in_=rope[0]), writes=["cos"], dsem="D_cos")
                P.op("sync", lambda e: e.dma_start(out=sin[:], in_=rope[1]), writes=["sin"], dsem="D_sin")
                cnt = {"wq": 0, "wv": 0, "q": 0, "r": 0, "qs": 0, "t": 0}
                for p in range(NTOK // TT):
                    r0 = p * TT
                    sq = r0 // S
                    pos0 = r0 % S
                    for t in range(NT):
                        sl = t % 2
                        P.op("sync", lambda e, t=t, sl=sl, r0=r0: e.dma_start(out=xt[sl][:], in_=src[r0 + t * 128:r0 + (t + 1) * 128, :]),
                             writes=[f"xt{sl}"], dsem=f"D_x{sl}")
                        norm_tile(ev, xt[sl][:], f"xt{sl}", ev.g[:], hT, f"hT{t}", t * 128, t)
                    pendq = []
                    for hd in range(NH):
                        for qk in range(2):
                            ws = cnt["wq"] % 4
                            cnt["wq"] += 1
                            c0 = qk * D + hd * 128
                            P.op("gpsimd", lambda e, ws=ws, c0=c0: e.dma_start(
                                out=wq[ws][:], in_=w_qkv[lj][:, c0:c0 + 128].rearrange("(c p) f -> p c f", p=128)),
                                writes=[f"wq{ws}"], dsem=f"D_wq{ws}")
                            qi = cnt["qs"] % 4
                            cnt["qs"] += 1
                            for hf in range(2):
                                bk = cnt["q"] % 3
                                cnt["q"] += 1
                                for k in range(DC):
                                    P.op("tensor", lambda e, bk=bk, ws=ws, k=k, hf=hf: e.matmul(
                                        psQ[bk][:], lhsT=wq[ws][:, k, :], rhs=hT[:, k, hf * 512:(hf + 1) * 512],
                                        start=(k == 0), stop=(k == DC - 1)),
                                        reads=[f"wq{ws}"] + [f"hT{t}" for t in range(hf * 4, hf * 4 + 4)],
                                        writes=[f"psQ{bk}"], sig=(k == DC - 1))
                                qv = qs[qi][:, hf * 512:(hf + 1) * 512]
                                P.op("scalar", lambda e, bk=bk, qv=qv: e.copy(out=qv, in_=psQ[bk][:]),
                                     reads=[f"psQ{bk}"], writes=[f"qs{qi}h{hf}"])
                                def fin(bk=bk, qi=qi, hf=hf, qk=qk, hd=hd):
                                    rb = cnt["r"] % 2
                                    cnt["r"] += 1
                                    P.op("tensor", lambda e: e.matmul(
                                        psR[rb][:], lhsT=Rt, rhs=qs[qi][:, hf * 512:(hf + 1) * 512], start=True, stop=True),
                                        reads=[f"qs{qi}h{hf}", "cs"], writes=[f"psR{rb}"])
                                    ti = cnt["t"] % 2
                                    cnt["t"] += 1
                                    pc = pos0 + hf * 512
                                    P.op("vector", lambda e: e.tensor_tensor(
                                        out=t1[ti][:], in0=psQ[bk][:], in1=cos[:, pc:pc + 512], op=ALU.mult),
                                        reads=[f"psQ{bk}", "cos", f"qs{qi}h{hf}"], writes=[f"t1{ti}"])
                                    P.op("vector", lambda e: e.tensor_tensor(
                                        out=t2[ti][:], in0=psR[rb][:], in1=sin[:, pc:pc + 512], op=ALU.mult),
                                        reads=[f"psR{rb}", "sin"], writes=[f"t2{ti}"])
                                    P.op("vector", lambda e: e.tensor_tensor(
                                        out=qs[qi][:, hf * 512:(hf + 1) * 512], in0=t1[ti][:], in1=t2[ti][:], op=ALU.add),
                                        reads=[f"t1{ti}", f"t2{ti}"], writes=[f"qs{qi}h{hf}"])
                                    if hf == 1:
                                        dsc = qsc if qk == 0 else ksc
                                        P.op("sync", lambda e, sq=sq, pos0=pos0: e.dma_start(
                                            out=dsc[sq, hd, :, pos0:pos0 + TT], in_=qs[qi][:]),
                                            reads=[f"qs{qi}h0", f"qs{qi}h1"], writes=[f"qk{qk}_{sq}_{hd}_{pos0}"], dsem=f"D_qs{qi}")
                                pendq.append(fin)
                                if len(pendq) > 1:
                                    pendq.pop(0)()
                    while pendq:
                        pendq.pop(0)()
                    for n in range(4):
                        ws = cnt["wv"] % 2
                        cnt["wv"] += 1
                        c0 = 2 * D + n * 512
                        P.op("gpsimd", lambda e, ws=ws, c0=c0: e.dma_start(
                            out=wv[ws][:], in_=w_qkv[lj][:, c0:c0 + 512].rearrange("(c p) f -> p c f", p=128)),
                            writes=[f"wv{ws}"], dsem=f"D_wv{ws}")
                        for t in range(NT):
                            bk = cnt["q"] % 3
                            cnt["q"] += 1
                            for k in range(DC):
                                P.op("tensor", lambda e, bk=bk, ws=ws, k=k, t=t: e.matmul(
                                    psQ[bk][:], lhsT=hT[:, k, t * 128:(t + 1) * 128], rhs=wv[ws][:, k, :],
                                    start=(k == 0), stop=(k == DC - 1)),
                                    reads=[f"wv{ws}", f"hT{t}"], writes=[f"psQ{bk}"], sig=(k == DC - 1))
                            eng = "scalar" if t % 2 == 0 else "vector"
                            if eng == "scalar":
                                P.op("scalar", lambda e, bk=bk, t=t, n=n: e.copy(out=vs[:, t, n * 512:(n + 1) * 512], in_=psQ[bk][:]),
                                     reads=[f"psQ{bk}"], writes=[f"vs{t}"])
                            else:
                                P.op("vector", lambda e, bk=bk, t=t, n=n: e.tensor_copy(out=vs[:, t, n * 512:(n + 1) * 512], in_=psQ[bk][:]),
                                     reads=[f"psQ{bk}"], writes=[f"vs{t}"])
                    for t in range(NT):
                        P.op("sync", lambda e, t=t, r0=r0: e.dma_start(out=vsc[r0 + t * 128:r0 + (t + 1) * 128, :], in_=vs[:, t, :]),
                             reads=[f"vs{t}"], writes=[f"vsc{r0 + t * 128}"], dsem=f"D_vs{t}")
                P.barrier()

        def phase_att_core():
            SC = 128.0 ** -0.5
            with contextlib.ExitStack() as ps:
                qd = {}
                kd = {}
                vd = {}
                for par in range(2):
                    for d in (1, 4, 16):
                        qd[par, d] = sbt(ps, f"q{d}_{par}", [128, S], BF16)
                        kd[par, d] = sbt(ps, f"k{d}_{par}", [128, S], BF16)
                        vd[par, d] = sbt(ps, f"v{d}_{par}", [128, 16, 128], BF16)
                nacc = [sbt(ps, f"nacc{i}", [128, S], F32) for i in range(2)]
                dacc = [sbt(ps, f"dacc{i}", [128, S], F32) for i in range(2)]
                osb = [sbt(ps, f"osb{i}", [128, S], BF16) for i in range(2)]
                PT = [sbt(ps, f"PT{i}", [128, 512], BF16) for i in range(4)]
                mk4 = sbt(ps, "mk4", [128, 512], BF16)
                psS = [pst(ps, f"psS{i}", [128, 512]) for i in range(4)]
                psN = [pst(ps, f"psN{i}", [128, 512]) for i in range(2)]
                psD = [pst(ps, f"psD{i}", [128, 512]) for i in range(2)]
                P.op("gpsimd", lambda e: e.tensor_copy(out=mk4[:, 0:256], in_=mask2), reads=["cs"], writes=["mk4"])
                P.op("gpsimd", lambda e: e.tensor_copy(out=mk4[:, 256:512], in_=mask2), reads=["cs"], writes=["mk4"])
                cnt = {"s": 0, "g": 0}
                heads = [(sq, hd) for sq in range(nseq) for hd in range(NH)]

                def loads(hi):
                    sq, hd = heads[hi]
                    par = hi % 2
                    kq = f"_{par}"
                    P.op("sync", lambda e: e.dma_start(out=qd[par, 1][:], in_=qsc[sq, hd]), writes=["q1" + kq], dsem="D_q" + kq)
                    P.op("sync", lambda e: e.dma_start(out=kd[par, 1][:], in_=ksc[sq, hd]), writes=["k1" + kq], dsem="D_k" + kq)
                    for d in (1, 4, 16):
                        for r in range(d):
                            nb = 16 // d
                            srcv = vsc[sq * S:(sq + 1) * S, hd * 128:(hd + 1) * 128].rearrange("(n p r) c -> p r n c", p=128, r=d)[:, r]
                            P.op("sync", lambda e, d=d, r=r, nb=nb, srcv=srcv: e.dma_start(
                                out=vd[par, d][:, r * nb:(r + 1) * nb, :], in_=srcv),
                                writes=[f"v{d}r{r}" + kq], dsem=f"D_v{d}" + kq)

                def deint(hi):
                    par = hi % 2
                    kq = f"_{par}"
                    for d in (4, 16):
                        P.op("scalar", lambda e, d=d: e.copy(
                            out=qd[par, d][:].rearrange("p (r j) -> p r j", r=d), in_=qd[par, 1][:].rearrange("p (j r) -> p r j", r=d)),
                            reads=["q1" + kq], writes=[f"q{d}" + kq])
                        P.op("vector", lambda e, d=d: e.tensor_copy(
                            out=kd[par, d][:].rearrange("p (r j) -> p r j", r=d), in_=kd[par, 1][:].rearrange("p (j r) -> p r j", r=d)),
                            reads=["k1" + kq], writes=[f"k{d}" + kq])

                loads(0)
                deint(0)
                for hi, (sq, hd) in enumerate(heads):
                    par = hi % 2
                    kq = f"_{par}"
                    if hi + 1 < len(heads):
                        loads(hi + 1)
                    for bi, d in enumerate((1, 4, 16)):
                        L = S // d
                        nbl = L // 128
                        qa, ka, va = qd[par, d], kd[par, d], vd[par, d]
                        rk = [f"q{d}" + kq, f"k{d}" + kq]
                        vkeys = [f"v{d}r{r}" + kq for r in range(d)]

                        def smm(pp, qa=qa, ka=ka, nbl=nbl, rk=rk):
                            sb_ = cnt["s"] % 4
                            cnt["s"] += 1
                            hps = []
                            for w in range(2):
                                m = 2 * pp + w
                                hp = (m % nbl) != 0
                                hps.append(hp)
                                if hp:
                                    P.op("tensor", lambda e, m=m, w=w: e.matmul(
                                        psS[sb_][:, w * 256:w * 256 + 128], lhsT=ka[:, (m - 1) * 128:m * 128], rhs=qa[:, m * 128:(m + 1) * 128], start=True, stop=True),
                                        reads=rk, writes=[f"psS{sb_}"], sig=False)
                                P.op("tensor", lambda e, m=m, w=w: e.matmul(
                                    psS[sb_][:, w * 256 + 128:w * 256 + 256], lhsT=ka[:, m * 128:(m + 1) * 128], rhs=qa[:, m * 128:(m + 1) * 128], start=True, stop=True),
                                    reads=rk, writes=[f"psS{sb_}"], sig=(w == 1))
                            if hps[0] and hps[1]:
                                view = lambda t: t[:, 0:512]
                            elif hps[1]:
                                view = lambda t: t[:, 128:512]
                            else:
                                view = lambda t: t[:, :].rearrange("p (a b) -> p a b", a=2)[:, :, 128:256]
                            P.op("scalar", lambda e: e.activation(out=view(PT[sb_]), in_=view(psS[sb_]), func=AF.Exp, scale=SC),
                                 reads=[f"psS{sb_}"], writes=[f"PT{sb_}"])
                            meng = "gpsimd" if (cnt["s"] % 2) == 0 else "vector"
                            P.op(meng, lambda e: e.tensor_tensor(out=view(PT[sb_]), in0=view(PT[sb_]), in1=view(mk4), op=ALU.mult),
                                 reads=["mk4"], writes=[f"PT{sb_}"])
                            return hps, sb_

                        def pvmm(pp, hps, pi, gb, va=va, vkeys=vkeys):
                            for w in range(2):
                                m = 2 * pp + w
                                hp = hps[w]
                                co = (m % 4) * 128
                                last = (m % 4) == 3
                                for which, bank, key in ((0, psN[gb], f"psN{gb}"), (1, psD[gb], f"psD{gb}")):
                                    if hp:
                                        P.op("tensor", lambda e, bank=bank, which=which, m=m, co=co, w=w: e.matmul(
                                            bank[:, co:co + 128], lhsT=(va[:, m - 1, :] if which == 0 else ones), rhs=PT[pi][:, w * 256:w * 256 + 128],
                                            start=True, stop=False),
                                            reads=[f"PT{pi}", "cs"] + vkeys, writes=[key], sig=False)
                                    P.op("tensor", lambda e, bank=bank, which=which, m=m, co=co, hp=hp, w=w: e.matmul(
                                        bank[:, co:co + 128], lhsT=(va[:, m, :] if which == 0 else ones), rhs=PT[pi][:, w * 256 + 128:w * 256 + 256],
                                        start=(not hp), stop=True),
                                        reads=[f"PT{pi}", "cs"] + vkeys, writes=[key], sig=(which == 1 and w == 1))

                        pend = [smm(0), smm(1), smm(2)]
                        for pp in range(8):
                            cur = pend.pop(0)
                            if pp + 3 < 8:
                                pend.append(smm(pp + 3))
                            gb = cnt["g"] % 2
                            pvmm(pp, cur[0], cur[1], gb)
                            if pp % 2 == 1:
                                cnt["g"] += 1
                                m0 = 2 * pp - 2
                                for acc, bank, key, akey in ((nacc[par], psN[gb], f"psN{gb}", "nacc" + kq), (dacc[par], psD[gb], f"psD{gb}", "dacc" + kq)):
                                    av = acc[:].rearrange("p (j r) -> p r j", r=d)
                                    if d == 1:
                                        oview = av[:, 0, m0 * 128:m0 * 128 + 512]
                                        iview = bank[:]
                                    elif d == 4:
                                        oview = av[:, m0 // 4, :]
                                        iview = bank[:]
                                    else:
                                        oview = av[:, m0:m0 + 4, :]
                                        iview = bank[:].rearrange("p (a b) -> p a b", a=4)
                                    if bi == 0:
                                        P.op("scalar", lambda e, oview=oview, iview=iview: e.copy(out=oview, in_=iview),
                                             reads=[key], writes=[akey])
                                    else:
                                        P.op("vector", lambda e, oview=oview, iview=iview: e.tensor_tensor(out=oview, in0=iview, in1=oview, op=ALU.add),
                                             reads=[key], writes=[akey])
                        if bi == 1 and hi + 1 < len(heads):
                            deint(hi + 1)
                    P.op("scalar", lambda e, par=par: e.activation(out=dacc[par][:], in_=dacc[par][:], func=AF.Ln),
                         reads=[], writes=["dacc" + kq])
                    P.op("scalar", lambda e, par=par: e.activation(out=dacc[par][:], in_=dacc[par][:], func=AF.Exp, scale=-1.0),
                         reads=[], writes=["dacc" + kq])
                    P.op("vector", lambda e, par=par: e.tensor_tensor(out=osb[par][:], in0=nacc[par][:], in1=dacc[par][:], op=ALU.mult),
                         reads=["nacc" + kq, "dacc" + kq], writes=["osb" + kq])
                    P.op("sync", lambda e, par=par, sq=sq, hd=hd: e.dma_start(out=osc[sq, hd * 128:(hd + 1) * 128, :], in_=osb[par][:]),
                         reads=["osb" + kq], writes=[f"osc{sq}_{hd}"], dsem="D_os" + kq)
                P.barrier()

        def load_wo(stack, lj):
            wos = [sbt(stack, f"wos{i}", [128, NH, 512], BF16) for i in range(4)]
            for n in range(4):
                P.op("gpsimd", lambda e, n=n: e.dma_start(
                    out=wos[n][:], in_=w_o[lj][:, n * 512:(n + 1) * 512].rearrange("(c p) d -> p c d", p=128)),
                    writes=[f"wos{n}"], dsem=f"D_wo{n}")
            return wos

        def phase_att_out(lj, src, dst, wos):
            TT, NT = 1024, 8
            NPASS = NTOK // TT
            with contextlib.ExitStack() as ps:
                xacc = sbt(ps, "xacc", [128, NT, D], F32)
                oT = [sbt(ps, f"oT{i}", [128, NH, TT], BF16) for i in range(2)]
                psB = [pst(ps, f"psB{i}", [128, 512]) for i in range(4)]
                cnt = {"b": 0}

                def load_oT(p):
                    r0 = p * TT
                    sq, pos0, sl = r0 // S, r0 % S, p % 2
                    P.op("sync", lambda e: e.dma_start(
                        out=oT[sl][:], in_=osc[sq][:, pos0:pos0 + TT].rearrange("(h p) t -> p h t", p=128)),
                        writes=[f"oT{sl}"], dsem=f"D_oT{sl}")

                def load_x(t, r0):
                    P.op("sync", lambda e: e.dma_start(out=xacc[:, t, :], in_=src[r0 + t * 128:r0 + (t + 1) * 128, :]),
                         writes=[f"xacc{t}"], dsem=f"D_x{t}")

                load_oT(0)
                for t in range(NT):
                    load_x(t, 0)
                for p in range(NPASS):
                    r0 = p * TT
                    sl = p % 2
                    if p + 1 < NPASS:
                        load_oT(p + 1)
                    for t in range(NT):
                        for n in range(4):
                            bk = cnt["b"] % 4
                            cnt["b"] += 1
                            for k in range(NH):
                                P.op("tensor", lambda e, bk=bk, n=n, k=k, t=t, sl=sl: e.matmul(
                                    psB[bk][:], lhsT=oT[sl][:, k, t * 128:(t + 1) * 128], rhs=wos[n][:, k, :],
                                    start=(k == 0), stop=(k == NH - 1)),
                                    reads=[f"wos{n}", f"oT{sl}"], writes=[f"psB{bk}"], sig=(k == NH - 1))
                            P.op("vector", lambda e, bk=bk, t=t, n=n: e.tensor_tensor(
                                out=xacc[:, t, n * 512:(n + 1) * 512], in0=psB[bk][:], in1=xacc[:, t, n * 512:(n + 1) * 512], op=ALU.add),
                                reads=[f"psB{bk}"], writes=[f"xacc{t}"])
                        P.op("sync", lambda e, t=t, r0=r0: e.dma_start(out=dst[r0 + t * 128:r0 + (t + 1) * 128, :], in_=xacc[:, t, :]),
                             reads=[f"xacc{t}"], writes=[f"dst{r0 + t * 128}"], dsem=f"D_xs{t}")
                        if p + 1 < NPASS:
                            load_x(t, r0 + TT)
                P.barrier()

        oneb_t = st.enter_context(nc.sbuf_tensor("oneb", [128, 1], F32))
        P.op("vector", lambda e: e.memset(oneb_t[:], 1.0), writes=["oneb"])
        Env.oneb = oneb_t
        P.barrier()
        cur = x_in
        nsub = len(plan)
        for si, (kind, idx, gi) in enumerate(plan):
            lastsub = si == nsub - 1
            dst = out if lastsub else xres
            if kind == "mlp":
                phase_mlp(idx, gi, cur, dst, final_norm and lastsub)
            elif kind == "lru":
                phase_lru(idx, gi, cur, dst)
            elif kind == "att":
                phase_att_qkv(idx, gi, cur)
                with contextlib.ExitStack() as shared:
                    wos_res = load_wo(shared, idx)
                    phase_att_core()
                    phase_att_out(idx, cur, dst, wos_res)
            elif kind == "attq":
                phase_att_qkv(idx, gi, cur)
            elif kind == "attqc":
                phase_att_qkv(idx, gi, cur)
                phase_att_core()
            cur = dst
        P.final_waits = {"sync": {s: v for s, v in P.cnt.items() if v > 0}}
        with nc.Block() as block:
            P.emit(block)
    return nc


def full_plan():
    plan = []
    for i in range(4):
        j = i // 2
        plan.append(("lru" if i % 2 == 0 else "att", j, i))
        plan.append(("mlp", i, 4 + i))
    return plan


def host_consts():
    bf = ml_dtypes.bfloat16
    cst = np.zeros((128, 640), np.float32)
    cst[:, 0:128] = np.eye(128)
    cst[:, 128:256] = 1.0
    j = np.arange(128)[:, None]
    i = np.arange(128)[None, :]
    cst[:, 256:384] = (j >= i)
    cst[:, 384:512] = (j <= i)
    for m in range(32):
        if m < 16:
            cst[m + 16, 512 + m] = -1.0
        else:
            cst[m - 16, 512 + m] = 1.0
    inv = 500000.0 ** (-np.arange(0, 32, 2, dtype=np.float32) / 32)
    ang = np.arange(S, dtype=np.float32)[None, :] * np.concatenate([inv, inv]).astype(np.float32)[:, None]
    rope = np.zeros((2, 128, S), np.float32)
    rope[0] = 1.0
    rope[0, 0:32] = np.cos(ang)
    rope[1, 0:32] = np.sin(ang)
    return cst.astype(bf), rope


def prep_shared(inp):
    f = lambda a: np.ascontiguousarray(np.asarray(a, dtype=np.float32))
    gains = np.concatenate([f(inp["mix_norm"]), f(inp["mlp_norm"]), f(inp["final_norm"])[None]], axis=0)
    gains = np.ascontiguousarray(np.broadcast_to(gains[:, None, :], (9, 128, D)))
    vecs = np.concatenate([f(inp["lru_conv_w"]), f(inp["lru_conv_b"])[:, None], f(inp["lru_b_a"]).reshape(2, 1, DR),
                           f(inp["lru_b_x"]).reshape(2, 1, DR), f(inp["lru_lambda"])[:, None]], axis=1)
    lruvec = np.ascontiguousarray(vecs.reshape(2, 8, RC, 128).transpose(0, 3, 1, 2))
    cst, rope = host_consts()
    sh = {"gains": gains, "lruvec": lruvec, "cst": cst, "rope": rope}
    for k in ("mlp_w1", "mlp_w2", "lru_w_in", "lru_w_a", "lru_w_x", "lru_w_out", "attn_w_qkv", "attn_w_o"):
        sh[k] = f(inp[k])
    return sh


def kernel(**inputs):
    n = 8
    x = np.asarray(inputs["x"], dtype=np.float32)
    B = x.shape[0]
    nseq = B // n
    sh = prep_shared(inputs)
    nc = build(nseq, full_plan(), True)
    in_maps = []
    for c in range(n):
        m = dict(sh)
        m["x"] = np.ascontiguousarray(x[c * nseq:(c + 1) * nseq].reshape(nseq * S, D))
        in_maps.append(m)
    res = run_bass_kernel_spmd(nc, in_maps, core_ids=list(range(n)))
    outs = [np.asarray(r["out"], dtype=np.float32).reshape(nseq, S, D) for r in res.results]
    return np.concatenate(outs, axis=0)
```

```python
import contextlib
import os
import numpy as np
import ml_dtypes
import concourse.bass as bass
import concourse.mybir as mybir
from concourse.bass_utils import run_bass_kernel_spmd

F32 = mybir.dt.float32
BF16 = mybir.dt.bfloat16
AF = mybir.ActivationFunctionType
ALU = mybir.AluOpType

D = 2048
DC = 16
S = 2048
FF = 8192
DR = 2560
RC = 20
NH = 16
EPS = 1e-6
ENGS = ("sync", "scalar", "gpsimd", "vector", "tensor")


class Prog:
    def __init__(self, nc, stack):
        self.nc = nc
        self.stack = stack
        self.ops = {e: [] for e in ENGS}
        self.sem = {}
        self.cnt = {}
        self.last_w = {}
        self.readers = {}
        self.final_waits = {}
        for e in ("scalar", "gpsimd", "vector", "tensor"):
            self.newsem("E_" + e)

    def newsem(self, name):
        if name not in self.sem:
            self.sem[name] = self.stack.enter_context(self.nc.semaphore(name))
            self.cnt[name] = 0

    def op(self, eng, fn, reads=(), writes=(), dsem=None, sig=True):
        waits = {}
        own = "E_" + eng

        def addw(tok):
            if tok is None:
                return
            s, v = tok
            if s == own and eng == "tensor":
                return
            if waits.get(s, 0) < v:
                waits[s] = v

        for k in reads:
            addw(self.last_w.get(k))
        for k in writes:
            addw(self.last_w.get(k))
            for r in self.readers.get(k, ()):
                addw(r)
        if dsem is not None:
            self.newsem(dsem)
            sname, n = dsem, 16
            self.cnt[sname] += n
            tok = (sname, self.cnt[sname])
        elif sig:
            sname, n = own, 1
            self.cnt[sname] += 1
            tok = (sname, self.cnt[sname])
        else:
            sname, n = None, 0
            tok = (own, self.cnt[own] + 1)
        self.ops[eng].append((fn, waits, sname, n))
        for k in reads:
            self.readers.setdefault(k, []).append(tok)
        for k in writes:
            self.last_w[k] = tok
            self.readers[k] = []
        return tok

    def barrier(self):
        allw = {s: v for s, v in self.cnt.items() if v > 0}
        for e in ENGS:
            self.ops[e].append((None, dict(allw), None, 0))
        self.last_w = {}
        self.readers = {}

    def emit(self, block):
        prog = self

        def mk(ename):
            def body(eng):
                seen = {}
                for fn, waits, sname, n in prog.ops[ename]:
                    for s, v in sorted(waits.items()):
                        if seen.get(s, 0) < v:
                            eng.wait_ge(prog.sem[s], v)
                            seen[s] = v
                    if fn is None:
                        continue
                    ins = fn(eng)
                    if sname is not None:
                        ins.then_inc(prog.sem[sname], n)
                for s, v in sorted(prog.final_waits.get(ename, {}).items()):
                    if seen.get(s, 0) < v:
                        eng.wait_ge(prog.sem[s], v)
            return body

        for ename in ENGS:
            if prog.ops[ename] or prog.final_waits.get(ename):
                getattr(block, ename)(mk(ename))


class Env:
    pass


def build(nseq, plan, final_norm=True):
    NTOK = nseq * S
    nc = bass.Bass("TRN2", target_bir_lowering=False)
    dt_in = lambda name, shape, dt=F32: nc.dram_tensor(name, list(shape), dt, kind="ExternalInput").ap()
    x_in = dt_in("x", [NTOK, D])
    gains = dt_in("gains", [9, 128, D])
    w1 = dt_in("mlp_w1", [4, D, FF])
    w2 = dt_in("mlp_w2", [4, FF, D])
    w_in = dt_in("lru_w_in", [2, D, 2 * DR])
    lruvec = dt_in("lruvec", [2, 128, 8, RC])
    w_a = dt_in("lru_w_a", [2, 10, 256, 256])
    w_x = dt_in("lru_w_x", [2, 10, 256, 256])
    w_out = dt_in("lru_w_out", [2, DR, D])
    w_qkv = dt_in("attn_w_qkv", [2, D, 3 * D])
    w_o = dt_in("attn_w_o", [2, D, D])
    cst = dt_in("cst", [128, 640], BF16)
    rope = dt_in("rope", [2, 128, S])
    out = nc.dram_tensor("out", [NTOK, D], F32, kind="ExternalOutput").ap()
    xres = nc.dram_tensor("xres", [NTOK, D], F32).ap()
    qsc = nc.dram_tensor("qsc", [nseq, NH, 128, S], BF16).ap()
    ksc = nc.dram_tensor("ksc", [nseq, NH, 128, S], BF16).ap()
    vsc = nc.dram_tensor("vsc", [NTOK, D], BF16).ap()
    osc = nc.dram_tensor("osc", [nseq, D, S], BF16).ap()

    with contextlib.ExitStack() as st:
        P = Prog(nc, st)
        cs = st.enter_context(nc.sbuf_tensor("cs", [128, 640], BF16))
        P.op("sync", lambda e: e.dma_start(out=cs[:], in_=cst), writes=["cs"], dsem="D_cs")
        ident = cs[:, 0:128]
        ones = cs[:, 128:256]
        mask2 = cs[:, 256:512]
        Rt = cs[:, 512:640]

        uid = [0]

        def sbt(stack, name, shape, dt):
            uid[0] += 1
            return stack.enter_context(nc.sbuf_tensor(f"{name}_u{uid[0]}", list(shape), dt))

        def pst(stack, name, shape, dt=F32):
            uid[0] += 1
            return stack.enter_context(nc.psum_tensor(f"{name}_u{uid[0]}", list(shape), dt))

        def norm_tile(ev, xap, xkey, gap, hT, hkey, col0, i, res_scale=None):
            norm_pre(ev, xap, xkey, gap, i)
            norm_T(ev, hT, hkey, col0, i)

        def norm_pre(ev, xap, xkey, gap, i):
            sl = i % 2
            ss = ev.ss[:, sl:sl + 1]
            hb = ev.hb[sl]
            P.op("scalar", lambda e: e.activation(out=hb[:], in_=xap, func=AF.Square, accum_out=ss),
                 reads=[xkey], writes=[f"hb{sl}", f"ss{sl}"])
            P.op("scalar", lambda e: e.activation(out=ss, in_=ss, func=AF.Sqrt, scale=1.0 / D, bias=ev.epsb[:, 0:1]),
                 reads=[f"ss{sl}"], writes=[f"ss{sl}"])
            P.op("vector", lambda e: e.reciprocal(out=ss, in_=ss), reads=[f"ss{sl}"], writes=[f"ss{sl}"])
            P.op("vector", lambda e: e.scalar_tensor_tensor(out=hb[:], in0=xap, scalar=ss, in1=gap, op0=ALU.mult, op1=ALU.mult),
                 reads=[xkey, f"ss{sl}", "g"], writes=[f"hb{sl}"])

        def norm_T(ev, hT, hkey, col0, i):
            sl = i % 2
            hb = ev.hb[sl]
            if getattr(ev, "psTf", None) is not None:
                for q in range(4):
                    bank, bkey = ev.psTf[q]
                    for c in range(4):
                        cc = q * 4 + c
                        P.op("tensor", lambda e, bank=bank, c=c, cc=cc: e.matmul(
                            bank[:, c * 128:(c + 1) * 128], lhsT=hb[:, cc * 128:(cc + 1) * 128], rhs=ident, start=True, stop=True),
                            reads=[f"hb{sl}", "cs"], writes=[bkey], sig=(c == 3))
                    if q % 2 == 0:
                        P.op("scalar", lambda e, bank=bank, q=q: e.copy(
                            out=hT[:, q * 4:(q + 1) * 4, col0:col0 + 128], in_=bank[:].rearrange("p (a b) -> p a b", a=4)),
                            reads=[bkey], writes=[hkey])
                    else:
                        P.op("vector", lambda e, bank=bank, q=q: e.tensor_copy(
                            out=hT[:, q * 4:(q + 1) * 4, col0:col0 + 128], in_=bank[:].rearrange("p (a b) -> p a b", a=4)),
                            reads=[bkey], writes=[hkey])
                return
            for hf in range(2):
                pT = ev.psT[hf]
                for c in range(8):
                    cc = hf * 8 + c
                    P.op("tensor", lambda e, pT=pT, c=c, cc=cc: e.transpose(out=pT[:, c, :], in_=hb[:, cc * 128:(cc + 1) * 128], identity=ident),
                         reads=[f"hb{sl}", "cs"], writes=[f"psT{hf}"], sig=(c == 7))
                eng = "scalar" if hf == 0 else "vector"
                if eng == "scalar":
                    P.op("scalar", lambda e, pT=pT, hf=hf: e.copy(out=hT[:, hf * 8:(hf + 1) * 8, col0:col0 + 128], in_=pT[:]),
                         reads=[f"psT{hf}"], writes=[hkey])
                else:
                    P.op("vector", lambda e, pT=pT, hf=hf: e.tensor_copy(out=hT[:, hf * 8:(hf + 1) * 8, col0:col0 + 128], in_=pT[:]),
                         reads=[f"psT{hf}"], writes=[hkey])

        def norm_bufs(ev, ps, gi, own_psT=True):
            ev.hb = [sbt(ps, f"hb{i}", [128, D], BF16) for i in range(2)]
            ev.ss = sbt(ps, "ss", [128, 2], F32)
            ev.epsb = sbt(ps, "epsb", [128, 1], F32)
            ev.g = sbt(ps, "g", [128, D], F32)
            ev.psTf = None
            if own_psT:
                ev.psT = [pst(ps, f"psT{i}", [128, 8, 128], BF16) for i in range(2)]
            P.op("vector", lambda e: e.memset(ev.epsb[:], EPS), writes=["epsb"])
            P.op("sync", lambda e: e.dma_start(out=ev.g[:], in_=gains[gi]), writes=["g"], dsem="D_g")

        def phase_mlp(li, gi, src, dst, fin):
            TT, NT = 1024, 8
            with contextlib.ExitStack() as ps:
                ev = Env()
                norm_bufs(ev, ps, gi)
                xacc = sbt(ps, "xacc", [128, NT, D], F32)
                hT = sbt(ps, "hT", [128, DC, TT], BF16)
                uT = [sbt(ps, f"uT{i}", [128, 8, TT], BF16) for i in range(2)]
                w1s = [sbt(ps, f"w1s{i}", [128, DC, 256], BF16) for i in range(2)]
                w2s = [sbt(ps, f"w2s{i}", [128, 8, 512], BF16) for i in range(4)]
                rt = [sbt(ps, f"rt{i}", [128, 512], F32) for i in range(2)]
                psA = [pst(ps, f"psA{i}", [128, 512]) for i in range(2)]
                psB = [pst(ps, f"psB{i}", [128, 512]) for i in range(4)]
                ssfin = sbt(ps, "ssfin", [128, 2], F32)
                if fin:
                    gf = sbt(ps, "gf", [128, D], F32)
                    P.op("sync", lambda e: e.dma_start(out=gf[:], in_=gains[8]), writes=["gf"], dsem="D_gf")
                cnt = {"w1": 0, "w2": 0, "a": 0, "b": 0, "r": 0}
                NPASS = NTOK // TT

                def load_tile(t, r0):
                    P.op("sync", lambda e, t=t, r0=r0: e.dma_start(out=xacc[:, t, :], in_=src[r0 + t * 128:r0 + (t + 1) * 128, :]),
                         writes=[f"xacc{t}"], dsem=f"D_x{t}")

                def finish_tile(t, r0):
                    if fin:
                        sl = t % 2
                        ss = ev.ss[:, sl:sl + 1]
                        ssf = ssfin[:, sl:sl + 1]
                        P.op("scalar", lambda e, t=t, ssf=ssf: e.activation(
                            out=uT[0][:, 0:2, :], in_=xacc[:, t, :].rearrange("p (a b) -> p a b", a=2), func=AF.Square, accum_out=ssf),
                            reads=[f"xacc{t}"], writes=["uT0h0", "uT0h1", f"ssf{sl}"])
                        P.op("scalar", lambda e, ssf=ssf: e.activation(out=ssf, in_=ssf, func=AF.Sqrt, scale=1.0 / D, bias=ev.epsb[:, 0:1]),
                             reads=[f"ssf{sl}"], writes=[f"ssf{sl}"])
                        P.op("vector", lambda e, ssf=ssf: e.reciprocal(out=ssf, in_=ssf), reads=[f"ssf{sl}"], writes=[f"ssf{sl}"])
                        P.op("vector", lambda e, t=t, ssf=ssf: e.scalar_tensor_tensor(
                            out=xacc[:, t, :], in0=xacc[:, t, :], scalar=ssf, in1=gf[:], op0=ALU.mult, op1=ALU.mult),
                            reads=[f"ssf{sl}", "gf"], writes=[f"xacc{t}"])
                    P.op("sync", lambda e, t=t, r0=r0: e.dma_start(out=dst[r0 + t * 128:r0 + (t + 1) * 128, :], in_=xacc[:, t, :]),
                         reads=[f"xacc{t}"], writes=[f"dst{r0 + t * 128}"], dsem=f"D_xs{t}")

                for p in range(NPASS):
                    r0 = p * TT
                    if p == 0:
                        for t in range(NT):
                            load_tile(t, r0)
                        for t in range(NT):
                            norm_tile(ev, xacc[:, t, :], f"xacc{t}", ev.g[:], hT, f"hT{t}", t * 128, t)

                    def stepA(gidx):
                        sl = gidx % 2
                        for pc in range(4):
                            ws = cnt["w1"] % 2
                            cnt["w1"] += 1
                            f0 = (gidx * 8 + pc * 2) * 128
                            P.op("gpsimd", lambda e, ws=ws, f0=f0: e.dma_start(
                                out=w1s[ws][:], in_=w1[li][:, f0:f0 + 256].rearrange("(c p) f -> p c f", p=128)),
                                writes=[f"w1s{ws}"], dsem=f"D_w1{ws}")
                            for j in range(2):
                                fc = pc * 2 + j
                                for hf in range(2):
                                    bk = cnt["a"] % 2
                                    cnt["a"] += 1
                                    for k in range(DC):
                                        P.op("tensor", lambda e, bk=bk, ws=ws, k=k, j=j, hf=hf: e.matmul(
                                            psA[bk][:], lhsT=w1s[ws][:, k, j * 128:(j + 1) * 128], rhs=hT[:, k, hf * 512:(hf + 1) * 512],
                                            start=(k == 0), stop=(k == DC - 1)),
                                            reads=[f"w1s{ws}"] + [f"hT{t}" for t in range(hf * 4, hf * 4 + 4)],
                                            writes=[f"psA{bk}"], sig=(k == DC - 1))
                                    ri = cnt["r"] % 2
                                    cnt["r"] += 1
                                    P.op("scalar", lambda e, bk=bk, ri=ri: e.activation(out=rt[ri][:], in_=psA[bk][:], func=AF.Relu),
                                         reads=[f"psA{bk}"], writes=[f"rt{ri}"])
                                    P.op("vector", lambda e, ri=ri, sl=sl, fc=fc, hf=hf: e.tensor_tensor(
                                        out=uT[sl][:, fc, hf * 512:(hf + 1) * 512], in0=rt[ri][:], in1=rt[ri][:], op=ALU.mult),
                                        reads=[f"rt{ri}"], writes=[f"uT{sl}h{hf}"])

                    def stepB(gidx):
                        sl = gidx % 2
                        for n in range(4):
                            ws = cnt["w2"] % 4
                            cnt["w2"] += 1
                            P.op("gpsimd", lambda e, ws=ws, n=n: e.dma_start(
                                out=w2s[ws][:], in_=w2[li][gidx * 1024:(gidx + 1) * 1024, n * 512:(n + 1) * 512].rearrange("(c p) d -> p c d", p=128)),
                                writes=[f"w2s{ws}"], dsem=f"D_w2{ws}")
                            for t in range(NT):
                                bk = cnt["b"] % 4
                                cnt["b"] += 1
                                for k in range(8):
                                    P.op("tensor", lambda e, bk=bk, ws=ws, k=k, t=t: e.matmul(
                                        psB[bk][:], lhsT=uT[sl][:, k, t * 128:(t + 1) * 128], rhs=w2s[ws][:, k, :],
                                        start=(k == 0), stop=(k == 7)),
                                        reads=[f"w2s{ws}", f"uT{sl}h{t // 4}"], writes=[f"psB{bk}"], sig=(k == 7))
                                P.op("vector", lambda e, bk=bk, t=t, n=n: e.tensor_tensor(
                                    out=xacc[:, t, n * 512:(n + 1) * 512], in0=psB[bk][:], in1=xacc[:, t, n * 512:(n + 1) * 512], op=ALU.add),
                                    reads=[f"psB{bk}"], writes=[f"xacc{t}"])

                    def stepB_last(gidx, prefetch):
                        sl = gidx % 2
                        wsl = []
                        for n in range(4):
                            ws = cnt["w2"] % 4
                            cnt["w2"] += 1
                            P.op("gpsimd", lambda e, ws=ws, n=n: e.dma_start(
                                out=w2s[ws][:], in_=w2[li][gidx * 1024:(gidx + 1) * 1024, n * 512:(n + 1) * 512].rearrange("(c p) d -> p c d", p=128)),
                                writes=[f"w2s{ws}"], dsem=f"D_w2{ws}")
                            wsl.append(ws)
                        for t in range(NT):
                            for n in range(4):
                                ws = wsl[n]
                                bk = cnt["b"] % 4
                                cnt["b"] += 1
                                for k in range(8):
                                    P.op("tensor", lambda e, bk=bk, ws=ws, k=k, t=t: e.matmul(
                                        psB[bk][:], lhsT=uT[sl][:, k, t * 128:(t + 1) * 128], rhs=w2s[ws][:, k, :],
                                        start=(k == 0), stop=(k == 7)),
                                        reads=[f"w2s{ws}", f"uT{sl}h{t // 4}"], writes=[f"psB{bk}"], sig=(k == 7))
                                P.op("vector", lambda e, bk=bk, t=t, n=n: e.tensor_tensor(
                                    out=xacc[:, t, n * 512:(n + 1) * 512], in0=psB[bk][:], in1=xacc[:, t, n * 512:(n + 1) * 512], op=ALU.add),
                                    reads=[f"psB{bk}"], writes=[f"xacc{t}"])
                            finish_tile(t, r0)
                            if prefetch:
                                load_tile(t, r0 + TT)
                                if t >= 1:
                                    norm_pre(ev, xacc[:, t - 1, :], f"xacc{t - 1}", ev.g[:], t - 1)
                                if t >= 2:
                                    norm_T(ev, hT, f"hT{t - 2}", (t - 2) * 128, t - 2)
                        if prefetch:
                            norm_pre(ev, xacc[:, NT - 1, :], f"xacc{NT - 1}", ev.g[:], NT - 1)
                            norm_T(ev, hT, f"hT{NT - 2}", (NT - 2) * 128, NT - 2)
                            norm_T(ev, hT, f"hT{NT - 1}", (NT - 1) * 128, NT - 1)

                    stepA(0)
                    for gidx in range(8):
                        if gidx + 1 < 8:
                            stepA(gidx + 1)
                        if gidx < 7:
                            stepB(gidx)
                        else:
                            stepB_last(gidx, p + 1 < NPASS)
                P.barrier()

        def phase_lru(lj, gi, src, dst):
            TT, NT = 512, 4
            with contextlib.ExitStack() as ps:
                ev = Env()
                norm_bufs(ev, ps, gi, own_psT=False)
                xacc = sbt(ps, "xacc", [128, NT, D], F32)
                hT = sbt(ps, "hT", [128, DC, TT], BF16)
                yT = sbt(ps, "yT", [128, RC, TT], BF16)
                wis = [sbt(ps, f"wis{i}", [128, DC, 256], BF16) for i in range(4)]
                wg = [sbt(ps, f"wg{i}", [128, 10, 2, 256], BF16) for i in range(2)]
                wos = [sbt(ps, f"wos{i}", [128, RC, 256], BF16) for i in range(2)]
                vec = sbt(ps, "vec", [128, 8, RC], F32)
                nsp = sbt(ps, "nsp", [128, RC], F32)
                nsp2 = sbt(ps, "nsp2", [128, RC], F32)
                hst = sbt(ps, "hst", [128, RC], F32)
                halo = sbt(ps, "halo", [128, RC, 4], F32)
                tA = sbt(ps, "tA", [128, 2, 516], F32)
                tB = [sbt(ps, f"tB{i}", [128, 2, 512], F32) for i in range(2)]
                tH = [sbt(ps, f"tH{i}", [128, 2, 512], F32) for i in range(2)]
                tD = sbt(ps, "tD", [128, 2, 512], F32)
                tE = sbt(ps, "tE", [128, 2, 512], F32)
                tF = sbt(ps, "tF", [128, 2, 512], F32)
                tG = sbt(ps, "tG", [128, 2, 512], F32)
                tI = sbt(ps, "tI", [128, 2, 512], F32)
                tC = sbt(ps, "tC", [128, 2, 512], BF16)
                psW = pst(ps, "psW", [128, 4, 512])
                psG = [pst(ps, f"psG{i}", [128, 512]) for i in range(4)]
                ev.psTf = [(psG[i], f"psG{i}") for i in range(4)]
                P.op("sync", lambda e: e.dma_start(out=vec[:], in_=lruvec[lj]), writes=["vec"], dsem="D_vec")
                for gt, wsrc in enumerate((w_a, w_x)):
                    for b in range(10):
                        P.op("gpsimd", lambda e, gt=gt, b=b, wsrc=wsrc: e.dma_start(
                            out=wg[gt][:, b, :, :], in_=wsrc[lj, b].rearrange("(c p) n -> p c n", p=128)),
                            writes=[f"wg{gt}b{b}"], dsem=f"D_wg{gt}")
                P.op("scalar", lambda e: e.activation(out=nsp[:], in_=vec[:, 7, :], func=AF.Exp, scale=-1.0), reads=["vec"], writes=["nsp"])
                P.op("scalar", lambda e: e.activation(out=nsp[:], in_=nsp[:], func=AF.Ln, bias=1.0), reads=["nsp"], writes=["nsp"])
                P.op("vector", lambda e: e.tensor_scalar(out=nsp2[:], in0=nsp[:], scalar1=-16.0, scalar2=None, op0=ALU.mult), reads=["nsp"], writes=["nsp2"])
                P.op("vector", lambda e: e.tensor_scalar(out=nsp[:], in0=nsp[:], scalar1=-8.0, scalar2=None, op0=ALU.mult), reads=["nsp", "nsp2"], writes=["nsp"])
                cnt = {"wi": 0, "g": 0, "wo": 0, "w": 0}

                def load_block_w(b):
                    res = []
                    for part in range(2):
                        ws = cnt["wi"] % 4
                        cnt["wi"] += 1
                        c0 = part * DR + b * 256
                        P.op("gpsimd", lambda e, ws=ws, c0=c0: e.dma_start(
                            out=wis[ws][:], in_=w_in[lj][:, c0:c0 + 256].rearrange("(c p) f -> p c f", p=128)),
                            writes=[f"wis{ws}"], dsem=f"D_wi{ws}")
                        res.append(ws)
                    return res

                for sq in range(nseq):
                    P.op("vector", lambda e: e.memset(hst[:], 0.0), writes=["hst"])
                    P.op("vector", lambda e: e.memset(halo[:], 0.0), writes=["halo"])
                    for ch in range(S // TT):
                        r0 = sq * S + ch * TT
                        for t in range(NT):
                            P.op("sync", lambda e, t=t, r0=r0: e.dma_start(out=xacc[:, t, :], in_=src[r0 + t * 128:r0 + (t + 1) * 128, :]),
                                 writes=[f"xacc{t}"], dsem=f"D_x{t}")
                        wslots = {0: load_block_w(0)}
                        for t in range(NT):
                            norm_tile(ev, xacc[:, t, :], f"xacc{t}", ev.g[:], hT, "hT", t * 128, t)

                        def wmm(b):
                            slots = wslots[b]
                            for part in range(2):
                                ws = slots[part]
                                for jj in range(2):
                                    bk = part * 2 + jj
                                    for k in range(DC):
                                        P.op("tensor", lambda e, bk=bk, ws=ws, k=k, jj=jj: e.matmul(
                                            psW[:, bk, :], lhsT=wis[ws][:, k, jj * 128:(jj + 1) * 128], rhs=hT[:, k, :],
                                            start=(k == 0), stop=(k == DC - 1)),
                                            reads=[f"wis{ws}", "hT"], writes=["psWx" if part == 0 else "psWg"], sig=(k == DC - 1))

                        def ev_(b):
                            pb = b % 2
                            c0 = 2 * b
                            Ht = tH[pb]
                            P.op("vector", lambda e: e.tensor_copy(out=tA[:, :, 0:3], in_=halo[:, c0:c0 + 2, 0:3]), reads=["halo"], writes=["tA"])
                            P.op("scalar", lambda e: e.copy(out=tA[:, :, 3:515], in_=psW[:, 0:2, :]), reads=["psWx"], writes=["tA"])
                            P.op("scalar", lambda e: e.copy(out=Ht[:], in_=psW[:, 2:4, :]), reads=["psWg"], writes=[f"tH{pb}"])
                            P.op("vector", lambda e: e.tensor_copy(out=halo[:, c0:c0 + 2, 0:3], in_=tA[:, :, 512:515]), reads=["tA"], writes=["halo"])

                        def conv_(b):
                            pb = b % 2
                            c0 = 2 * b
                            Bt = tB[pb]
                            for jj in range(2):
                                c = c0 + jj
                                P.op("vector", lambda e, jj=jj, c=c: e.tensor_scalar(
                                    out=Bt[:, jj, :], in0=tA[:, jj, 3:515], scalar1=vec[:, 3, c:c + 1], scalar2=vec[:, 4, c:c + 1],
                                    op0=ALU.mult, op1=ALU.add),
                                    reads=["tA", "vec"], writes=[f"tB{pb}j{jj}"])
                            for k in range(3):
                                for jj in range(2):
                                    c = c0 + jj
                                    P.op("vector", lambda e, jj=jj, c=c, k=k: e.scalar_tensor_tensor(
                                        out=Bt[:, jj, :], in0=tA[:, jj, k:k + 512], scalar=vec[:, k, c:c + 1], in1=Bt[:, jj, :], op0=ALU.mult, op1=ALU.add),
                                        reads=["tA", "vec"], writes=[f"tB{pb}j{jj}"])
                            P.op("vector", lambda e: e.tensor_copy(out=tC[:], in_=Bt[:]), reads=[f"tB{pb}j0", f"tB{pb}j1"], writes=["tC"])

                        def gelupre_(b):
                            pb = b % 2
                            Ht = tH[pb]
                            P.op("vector", lambda e: e.tensor_tensor(out=tI[:], in0=Ht[:], in1=Ht[:], op=ALU.mult), reads=[f"tH{pb}"], writes=["tI"])
                            P.op("vector", lambda e: e.tensor_scalar(out=tI[:], in0=tI[:], scalar1=0.044715, scalar2=1.0, op0=ALU.mult, op1=ALU.add),
                                 reads=["tI"], writes=["tI"])
                            P.op("vector", lambda e: e.tensor_tensor(out=tI[:], in0=tI[:], in1=Ht[:], op=ALU.mult), reads=[f"tH{pb}"], writes=["tI"])

                        def gates(b, gt):
                            if True:
                                for jo in range(2):
                                    bk = cnt["g"] % 4
                                    cnt["g"] += 1
                                    c = 2 * b + jo
                                    for jj in range(2):
                                        P.op("tensor", lambda e, bk=bk, gt=gt, jo=jo, jj=jj: e.matmul(
                                            psG[bk][:], lhsT=wg[gt][:, b, jj, jo * 128:(jo + 1) * 128], rhs=tC[:, jj, :],
                                            start=(jj == 0), stop=(jj == 1)),
                                            reads=[f"wg{gt}b{bb}" for bb in range(10)] + ["tC"], writes=[f"psG{bk}"], sig=(jj == 1))
                                    dstt = tD if gt == 0 else tE
                                    nm = "D" if gt == 0 else "E"
                                    P.op("scalar", lambda e, bk=bk, dstt=dstt, gt=gt, c=c, jo=jo: e.activation(
                                        out=dstt[:, jo, :], in_=psG[bk][:], func=AF.Sigmoid, bias=vec[:, 5 + gt, c:c + 1]),
                                        reads=[f"psG{bk}", "vec"], writes=[f"t{nm}j{jo}"])

                        def st2(b):
                            pb = b % 2
                            c0 = 2 * b
                            Bt, Ht = tB[pb], tH[pb]
                            for jo in range(2):
                                c = c0 + jo
                                P.op("scalar", lambda e, jo=jo, c=c: e.activation(out=tF[:, jo, :], in_=tD[:, jo, :], func=AF.Exp, scale=nsp2[:, c:c + 1]),
                                     reads=[f"tDj{jo}", "nsp2"], writes=[f"tFj{jo}"])
                            for jo in range(2):
                                c = c0 + jo
                                P.op("scalar", lambda e, jo=jo, c=c: e.activation(out=tD[:, jo, :], in_=tD[:, jo, :], func=AF.Exp, scale=nsp[:, c:c + 1]),
                                     reads=[f"tDj{jo}", "nsp"], writes=[f"tDj{jo}"])
                            P.op("vector", lambda e: e.tensor_scalar(out=tF[:], in0=tF[:], scalar1=1.0, scalar2=None, op0=ALU.min),
                                 reads=["tFj0", "tFj1"], writes=["tFj0", "tFj1"])
                            P.op("scalar", lambda e: e.activation(out=tF[:], in_=tF[:], func=AF.Sqrt, scale=-1.0, bias=1.0),
                                 reads=["tFj0", "tFj1"], writes=["tFj0", "tFj1"])
                            P.op("scalar", lambda e: e.activation(out=tI[:], in_=tI[:], func=AF.Sigmoid, scale=1.5957691), reads=["tI"], writes=["tI"])
                            P.op("vector", lambda e: e.tensor_tensor(out=Bt[:], in0=Bt[:], in1=tE[:], op=ALU.mult),
                                 reads=["tEj0", "tEj1"], writes=[f"tB{pb}j0", f"tB{pb}j1"])
                            P.op("vector", lambda e: e.tensor_tensor(out=Bt[:], in0=Bt[:], in1=tF[:], op=ALU.mult),
                                 reads=["tFj0", "tFj1"], writes=[f"tB{pb}j0", f"tB{pb}j1"])
                            for jo in range(2):
                                c = c0 + jo
                                P.op("vector", lambda e, jo=jo, c=c: e.tensor_tensor_scan(
                                    out=tG[:, jo, :], data0=tD[:, jo, :], data1=Bt[:, jo, :], initial=hst[:, c:c + 1], op0=ALU.mult, op1=ALU.add),
                                    reads=[f"tDj{jo}", f"tB{pb}j{jo}", "hst"], writes=[f"tGj{jo}"])
                            P.op("vector", lambda e: e.tensor_copy(out=hst[:, c0:c0 + 2], in_=tG[:, :, 511]), reads=["tGj0", "tGj1"], writes=["hst"])
                            P.op("vector", lambda e: e.tensor_tensor(out=tI[:], in0=tI[:], in1=Ht[:], op=ALU.mult), reads=[f"tH{pb}"], writes=["tI"])
                            P.op("vector", lambda e: e.tensor_tensor(out=yT[:, c0:c0 + 2, :], in0=tG[:], in1=tI[:], op=ALU.mult),
                                 reads=["tGj0", "tGj1", "tI"], writes=["yT"])

                        wmm(0)
                        ev_(0)
                        conv_(0)
                        gelupre_(0)
                        for b in range(10):
                            more = b + 1 < 10
                            if more:
                                wslots[b + 1] = load_block_w(b + 1)
                                wmm(b + 1)
                            gates(b, 0)
                            if more:
                                ev_(b + 1)
                            gates(b, 1)
                            if more:
                                conv_(b + 1)
                            st2(b)
                            if more:
                                gelupre_(b + 1)
                        for n in range(8):
                            ws = cnt["wo"] % 2
                            cnt["wo"] += 1
                            P.op("gpsimd", lambda e, ws=ws, n=n: e.dma_start(
                                out=wos[ws][:], in_=w_out[lj][:, n * 256:(n + 1) * 256].rearrange("(c p) d -> p c d", p=128)),
                                writes=[f"wos{ws}"], dsem=f"D_wo{ws}")
                            for t in range(NT):
                                bk = cnt["w"] % 4
                                cnt["w"] += 1
                                for k in range(RC):
                                    P.op("tensor", lambda e, bk=bk, ws=ws, k=k, t=t: e.matmul(
                                        psW[:, bk, 0:256], lhsT=yT[:, k, t * 128:(t + 1) * 128], rhs=wos[ws][:, k, :],
                                        start=(k == 0), stop=(k == RC - 1)),
                                        reads=[f"wos{ws}", "yT"], writes=[f"psWo{bk}", "psWx", "psWg"], sig=(k == RC - 1))
                                P.op("vector", lambda e, bk=bk, t=t, n=n: e.tensor_tensor(
                                    out=xacc[:, t, n * 256:(n + 1) * 256], in0=psW[:, bk, 0:256], in1=xacc[:, t, n * 256:(n + 1) * 256], op=ALU.add),
                                    reads=[f"psWo{bk}"], writes=[f"xacc{t}"])
                        for t in range(NT):
                            P.op("sync", lambda e, t=t, r0=r0: e.dma_start(out=dst[r0 + t * 128:r0 + (t + 1) * 128, :], in_=xacc[:, t, :]),
                                 reads=[f"xacc{t}"], writes=[f"dst{r0 + t * 128}"], dsem=f"D_xs{t}")
                P.barrier()

        def phase_att_qkv(lj, gi, src):
            TT, NT = 1024, 8
            with contextlib.ExitStack() as ps:
                ev = Env()
                norm_bufs(ev, ps, gi)
                xt = [sbt(ps, f"xt{i}", [128, D], F32) for i in range(4)]
                hT = sbt(ps, "hT", [128, DC, TT], BF16)
                wq = [sbt(ps, f"wq{i}", [128, DC, 128], BF16) for i in range(4)]
                wv = [sbt(ps, f"wv{i}", [128, DC, 512], BF16) for i in range(2)]
                qs = [sbt(ps, f"qs{i}", [128, TT], BF16) for i in range(4)]
                t1 = [sbt(ps, f"t1{i}", [128, 512], F32) for i in range(2)]
                t2 = [sbt(ps, f"t2{i}", [128, 512], F32) for i in range(2)]
                cos = sbt(ps, "cos", [128, S], F32)
                sin = sbt(ps, "sin", [128, S], F32)
                vs = sbt(ps, "vs", [128, NT, D], BF16)
                psQ = [pst(ps, f"psQ{i}", [128, 512]) for i in range(3)]
                psR = [pst(ps, f"psR{i}", [128, 512]) for i in range(2)]
                P.op("sync", lambda e: e.dma_start(out=cos[:], in_=rope[0]), writes=["cos"], dsem="D_cos")
                P.op("sync", lambda e: e.dma_start(out=sin[:], in_=rope[1]), writes=["sin"], dsem="D_sin")
                cnt = {"wq": 0, "wv": 0, "q": 0, "r": 0, "qs": 0, "t": 0}
                for p in range(NTOK // TT):
                    r0 = p * TT
                    sq = r0 // S
                    pos0 = r0 % S
                    for t in range(NT):
                        sl = t % 4
                        P.op("sync", lambda e, t=t, sl=sl, r0=r0: e.dma_start(out=xt[sl][:], in_=src[r0 + t * 128:r0 + (t + 1) * 128, :]),
                             writes=[f"xt{sl}"], dsem=f"D_x{sl}")
                        if t >= 2:
                            tt = t - 2
                            norm_tile(ev, xt[tt % 4][:], f"xt{tt % 4}", ev.g[:], hT, f"hT{tt}", tt * 128, tt)
                    for tt in range(NT - 2, NT):
                        norm_tile(ev, xt[tt % 4][:], f"xt{tt % 4}", ev.g[:], hT, f"hT{tt}", tt * 128, tt)
                    pendq = []
                    for hd in range(NH):
                        for qk in range(2):
                            ws = cnt["wq"] % 4
                            cnt["wq"] += 1
                            c0 = qk * D + hd * 128
                            P.op("gpsimd", lambda e, ws=ws, c0=c0: e.dma_start(
                                out=wq[ws][:], in_=w_qkv[lj][:, c0:c0 + 128].rearrange("(c p) f -> p c f", p=128)),
                                writes=[f"wq{ws}"], dsem=f"D_wq{ws}")
                            qi = cnt["qs"] % 4
                            cnt["qs"] += 1
                            for hf in range(2):
                                bk = cnt["q"] % 3
                                cnt["q"] += 1
                                for k in range(DC):
                                    P.op("tensor", lambda e, bk=bk, ws=ws, k=k, hf=hf: e.matmul(
                                        psQ[bk][:], lhsT=wq[ws][:, k, :], rhs=hT[:, k, hf * 512:(hf + 1) * 512],
                                        start=(k == 0), stop=(k == DC - 1)),
                                        reads=[f"wq{ws}"] + [f"hT{t}" for t in range(hf * 4, hf * 4 + 4)],
                                        writes=[f"psQ{bk}"], sig=(k == DC - 1))
                                qv = qs[qi][:, hf * 512:(hf + 1) * 512]
                                P.op("scalar", lambda e, bk=bk, qv=qv: e.copy(out=qv, in_=psQ[bk][:]),
                                     reads=[f"psQ{bk}"], writes=[f"qs{qi}h{hf}"])
                                def fin(bk=bk, qi=qi, hf=hf, qk=qk, hd=hd):
                                    rb = cnt["r"] % 2
                                    cnt["r"] += 1
                                    P.op("tensor", lambda e: e.matmul(
                                        psR[rb][:], lhsT=Rt, rhs=qs[qi][:, hf * 512:(hf + 1) * 512], start=True, stop=True),
                                        reads=[f"qs{qi}h{hf}", "cs"], writes=[f"psR{rb}"])
                                    ti = cnt["t"] % 2
                                    cnt["t"] += 1
                                    pc = pos0 + hf * 512
                                    P.op("vector", lambda e: e.tensor_tensor(
                                        out=t1[ti][:], in0=psQ[bk][:], in1=cos[:, pc:pc + 512], op=ALU.mult),
                                        reads=[f"psQ{bk}", "cos", f"qs{qi}h{hf}"], writes=[f"t1{ti}"])
                                    P.op("vector", lambda e: e.tensor_tensor(
                                        out=t2[ti][:], in0=psR[rb][:], in1=sin[:, pc:pc + 512], op=ALU.mult),
                                        reads=[f"psR{rb}", "sin"], writes=[f"t2{ti}"])
                                    P.op("vector", lambda e: e.tensor_tensor(
                                        out=qs[qi][:, hf * 512:(hf + 1) * 512], in0=t1[ti][:], in1=t2[ti][:], op=ALU.add),
                                        reads=[f"t1{ti}", f"t2{ti}"], writes=[f"qs{qi}h{hf}"])
                                    if hf == 1:
                                        dsc = qsc if qk == 0 else ksc
                                        P.op("sync", lambda e, sq=sq, pos0=pos0: e.dma_start(
                                            out=dsc[sq, hd, :, pos0:pos0 + TT], in_=qs[qi][:]),
                                            reads=[f"qs{qi}h0", f"qs{qi}h1"], writes=[f"qk{qk}_{sq}_{hd}_{pos0}"], dsem=f"D_qs{qi}")
                                pendq.append(fin)
                                if len(pendq) > 1:
                                    pendq.pop(0)()
                    while pendq:
                        pendq.pop(0)()
                    for n in range(4):
                        ws = cnt["wv"] % 2
                        cnt["wv"] += 1
                        c0 = 2 * D + n * 512
                        P.op("gpsimd", lambda e, ws=ws, c0=c0: e.dma_start(
                            out=wv[ws][:], in_=w_qkv[lj][:, c0:c0 + 512].rearrange("(c p) f -> p c f", p=128)),
                            writes=[f"wv{ws}"], dsem=f"D_wv{ws}")
                        for t in range(NT):
                            bk = cnt["q"] % 3
                            cnt["q"] += 1
                            for k in range(DC):
                                P.op("tensor", lambda e, bk=bk, ws=ws, k=k, t=t: e.matmul(
                                    psQ[bk][:], lhsT=hT[:, k, t * 128:(t + 1) * 128], rhs=wv[ws][:, k, :],
                                    start=(k == 0), stop=(k == DC - 1)),
                                    reads=[f"wv{ws}", f"hT{t}"], writes=[f"psQ{bk}"], sig=(k == DC - 1))
                            eng = "scalar" if t % 2 == 0 else "vector"
                            if eng == "scalar":
                                P.op("scalar", lambda e, bk=bk, t=t, n=n: e.copy(out=vs[:, t, n * 512:(n + 1) * 512], in_=psQ[bk][:]),
                                     reads=[f"psQ{bk}"], writes=[f"vs{t}"])
                            else:
                                P.op("vector", lambda e, bk=bk, t=t, n=n: e.tensor_copy(out=vs[:, t, n * 512:(n + 1) * 512], in_=psQ[bk][:]),
                                     reads=[f"psQ{bk}"], writes=[f"vs{t}"])
                    for t in range(NT):
                        P.op("sync", lambda e, t=t, r0=r0: e.dma_start(out=vsc[r0 + t * 128:r0 + (t + 1) * 128, :], in_=vs[:, t, :]),
                             reads=[f"vs{t}"], writes=[f"vsc{r0 + t * 128}"], dsem=f"D_vs{t}")
                P.barrier()

        def phase_att_core():
            SC = 128.0 ** -0.5
            with contextlib.ExitStack() as ps:
                qd = {}
                kd = {}
                vd = {}
                for par in range(2):
                    for d in (1, 4, 16):
                        qd[par, d] = sbt(ps, f"q{d}_{par}", [128, S], BF16)
                        kd[par, d] = sbt(ps, f"k{d}_{par}", [128, S], BF16)
                        vd[par, d] = sbt(ps, f"v{d}_{par}", [128, 16, 128], BF16)
                nacc = [sbt(ps, f"nacc{i}", [128, S], F32) for i in range(2)]
                dacc = [sbt(ps, f"dacc{i}", [128, S], F32) for i in range(2)]
                osb = [sbt(ps, f"osb{i}", [128, S], BF16) for i in range(2)]
                PT = [sbt(ps, f"PT{i}", [128, 512], BF16) for i in range(4)]
                mk4 = sbt(ps, "mk4", [128, 512], BF16)
                psS = [pst(ps, f"psS{i}", [128, 512]) for i in range(4)]
                psN = [pst(ps, f"psN{i}", [128, 512]) for i in range(2)]
                psD = [pst(ps, f"psD{i}", [128, 512]) for i in range(2)]
                P.op("gpsimd", lambda e: e.tensor_copy(out=mk4[:, 0:256], in_=mask2), reads=["cs"], writes=["mk4"])
                P.op("gpsimd", lambda e: e.tensor_copy(out=mk4[:, 256:512], in_=mask2), reads=["cs"], writes=["mk4"])
                cnt = {"s": 0, "g": 0}
                heads = [(sq, hd) for sq in range(nseq) for hd in range(NH)]

                def loads(hi):
                    sq, hd = heads[hi]
                    par = hi % 2
                    kq = f"_{par}"
                    P.op("sync", lambda e: e.dma_start(out=qd[par, 1][:], in_=qsc[sq, hd]), writes=["q1" + kq], dsem="D_q" + kq)
                    P.op("sync", lambda e: e.dma_start(out=kd[par, 1][:], in_=ksc[sq, hd]), writes=["k1" + kq], dsem="D_k" + kq)
                    for d in (1, 4, 16):
                        for r in range(d):
                            nb = 16 // d
                            srcv = vsc[sq * S:(sq + 1) * S, hd * 128:(hd + 1) * 128].rearrange("(n p r) c -> p r n c", p=128, r=d)[:, r]
                            P.op("sync", lambda e, d=d, r=r, nb=nb, srcv=srcv: e.dma_start(
                                out=vd[par, d][:, r * nb:(r + 1) * nb, :], in_=srcv),
                                writes=[f"v{d}r{r}" + kq], dsem=f"D_v{d}" + kq)

                def deint(hi):
                    par = hi % 2
                    kq = f"_{par}"
                    for d in (4, 16):
                        P.op("scalar", lambda e, d=d: e.copy(
                            out=qd[par, d][:].rearrange("p (r j) -> p r j", r=d), in_=qd[par, 1][:].rearrange("p (j r) -> p r j", r=d)),
                            reads=["q1" + kq], writes=[f"q{d}" + kq])
                        P.op("vector", lambda e, d=d: e.tensor_copy(
                            out=kd[par, d][:].rearrange("p (r j) -> p r j", r=d), in_=kd[par, 1][:].rearrange("p (j r) -> p r j", r=d)),
                            reads=["k1" + kq], writes=[f"k{d}" + kq])

                loads(0)
                deint(0)
                for hi, (sq, hd) in enumerate(heads):
                    par = hi % 2
                    kq = f"_{par}"
                    if hi + 1 < len(heads):
                        loads(hi + 1)
                    for bi, d in enumerate((1, 4, 16)):
                        L = S // d
                        nbl = L // 128
                        qa, ka, va = qd[par, d], kd[par, d], vd[par, d]
                        rk = [f"q{d}" + kq, f"k{d}" + kq]
                        vkeys = [f"v{d}r{r}" + kq for r in range(d)]

                        def smm(pp, qa=qa, ka=ka, nbl=nbl, rk=rk):
                            sb_ = cnt["s"] % 4
                            cnt["s"] += 1
                            hps = []
                            for w in range(2):
                                m = 2 * pp + w
                                hp = (m % nbl) != 0
                                hps.append(hp)
                                if hp:
                                    P.op("tensor", lambda e, m=m, w=w: e.matmul(
                                        psS[sb_][:, w * 256:w * 256 + 128], lhsT=ka[:, (m - 1) * 128:m * 128], rhs=qa[:, m * 128:(m + 1) * 128], start=True, stop=True),
                                        reads=rk, writes=[f"psS{sb_}"], sig=False)
                                P.op("tensor", lambda e, m=m, w=w: e.matmul(
                                    psS[sb_][:, w * 256 + 128:w * 256 + 256], lhsT=ka[:, m * 128:(m + 1) * 128], rhs=qa[:, m * 128:(m + 1) * 128], start=True, stop=True),
                                    reads=rk, writes=[f"psS{sb_}"], sig=(w == 1))
                            if hps[0] and hps[1]:
                                view = lambda t: t[:, 0:512]
                            elif hps[1]:
                                view = lambda t: t[:, 128:512]
                            else:
                                view = lambda t: t[:, :].rearrange("p (a b) -> p a b", a=2)[:, :, 128:256]
                            P.op("scalar", lambda e: e.activation(out=view(PT[sb_]), in_=view(psS[sb_]), func=AF.Exp, scale=SC),
                                 reads=[f"psS{sb_}"], writes=[f"PT{sb_}"])
                            meng = "gpsimd" if (cnt["s"] % 2) == 0 else "vector"
                            P.op(meng, lambda e: e.tensor_tensor(out=view(PT[sb_]), in0=view(PT[sb_]), in1=view(mk4), op=ALU.mult),
                                 reads=["mk4"], writes=[f"PT{sb_}"])
                            return hps, sb_

                        def pvmm(pp, hps, pi, gb, va=va, vkeys=vkeys):
                            for w in range(2):
                                m = 2 * pp + w
                                hp = hps[w]
                                co = (m % 4) * 128
                                last = (m % 4) == 3
                                for which, bank, key in ((0, psN[gb], f"psN{gb}"), (1, psD[gb], f"psD{gb}")):
                                    if hp:
                                        P.op("tensor", lambda e, bank=bank, which=which, m=m, co=co, w=w: e.matmul(
                                            bank[:, co:co + 128], lhsT=(va[:, m - 1, :] if which == 0 else ones), rhs=PT[pi][:, w * 256:w * 256 + 128],
                                            start=True, stop=False),
                                            reads=[f"PT{pi}", "cs"] + vkeys, writes=[key], sig=False)
                                    P.op("tensor", lambda e, bank=bank, which=which, m=m, co=co, hp=hp, w=w: e.matmul(
                                        bank[:, co:co + 128], lhsT=(va[:, m, :] if which == 0 else ones), rhs=PT[pi][:, w * 256 + 128:w * 256 + 256],
                                        start=(not hp), stop=True),
                                        reads=[f"PT{pi}", "cs"] + vkeys, writes=[key], sig=(which == 1 and w == 1))

                        pend = [smm(0), smm(1), smm(2)]
                        for pp in range(8):
                            cur = pend.pop(0)
                            if pp + 3 < 8:
                                pend.append(smm(pp + 3))
                            gb = cnt["g"] % 2
                            pvmm(pp, cur[0], cur[1], gb)
                            if pp % 2 == 1:
                                cnt["g"] += 1
                                m0 = 2 * pp - 2
                                for acc, bank, key, akey in ((nacc[par], psN[gb], f"psN{gb}", "nacc" + kq), (dacc[par], psD[gb], f"psD{gb}", "dacc" + kq)):
                                    av = acc[:].rearrange("p (j r) -> p r j", r=d)
                                    if d == 1:
                                        oview = av[:, 0, m0 * 128:m0 * 128 + 512]
                                        iview = bank[:]
                                    elif d == 4:
                                        oview = av[:, m0 // 4, :]
                                        iview = bank[:]
                                    else:
                                        oview = av[:, m0:m0 + 4, :]
                                        iview = bank[:].rearrange("p (a b) -> p a b", a=4)
                                    if bi == 0:
                                        P.op("scalar", lambda e, oview=oview, iview=iview: e.copy(out=oview, in_=iview),
                                             reads=[key], writes=[akey])
                                    else:
                                        P.op("vector", lambda e, oview=oview, iview=iview: e.tensor_tensor(out=oview, in0=iview, in1=oview, op=ALU.add),
                                             reads=[key], writes=[akey])
                        if bi == 1 and hi + 1 < len(heads):
                            deint(hi + 1)
                    P.op("scalar", lambda e, par=par: e.activation(out=dacc[par][:], in_=dacc[par][:], func=AF.Ln),
                         reads=[], writes=["dacc" + kq])
                    P.op("scalar", lambda e, par=par: e.activation(out=dacc[par][:], in_=dacc[par][:], func=AF.Exp, scale=-1.0),
                         reads=[], writes=["dacc" + kq])
                    P.op("vector", lambda e, par=par: e.tensor_tensor(out=osb[par][:], in0=nacc[par][:], in1=dacc[par][:], op=ALU.mult),
                         reads=["nacc" + kq, "dacc" + kq], writes=["osb" + kq])
                    P.op("sync", lambda e, par=par, sq=sq, hd=hd: e.dma_start(out=osc[sq, hd * 128:(hd + 1) * 128, :], in_=osb[par][:]),
                         reads=["osb" + kq], writes=[f"osc{sq}_{hd}"], dsem="D_os" + kq)
                P.barrier()

        def load_wo(stack, lj):
            wos = [sbt(stack, f"wos{i}", [128, NH, 512], BF16) for i in range(4)]
            for n in range(4):
                P.op("gpsimd", lambda e, n=n: e.dma_start(
                    out=wos[n][:], in_=w_o[lj][:, n * 512:(n + 1) * 512].rearrange("(c p) d -> p c d", p=128)),
                    writes=[f"wos{n}"], dsem=f"D_wo{n}")
            return wos

        def phase_att_out(lj, src, dst, wos):
            TT, NT = 1024, 8
            NPASS = NTOK // TT
            with contextlib.ExitStack() as ps:
                xacc = sbt(ps, "xacc", [128, NT, D], F32)
                oT = [sbt(ps, f"oT{i}", [128, NH, TT], BF16) for i in range(2)]
                psB = [pst(ps, f"psB{i}", [128, 512]) for i in range(4)]
                cnt = {"b": 0}

                def load_oT(p):
                    r0 = p * TT
                    sq, pos0, sl = r0 // S, r0 % S, p % 2
                    P.op("sync", lambda e: e.dma_start(
                        out=oT[sl][:], in_=osc[sq][:, pos0:pos0 + TT].rearrange("(h p) t -> p h t", p=128)),
                        writes=[f"oT{sl}"], dsem=f"D_oT{sl}")

                def load_x(t, r0):
                    P.op("sync", lambda e: e.dma_start(out=xacc[:, t, :], in_=src[r0 + t * 128:r0 + (t + 1) * 128, :]),
                         writes=[f"xacc{t}"], dsem=f"D_x{t}")

                load_oT(0)
                for t in range(NT):
                    load_x(t, 0)
                for p in range(NPASS):
                    r0 = p * TT
                    sl = p % 2
                    if p + 1 < NPASS:
                        load_oT(p + 1)
                    for t in range(NT):
                        for n in range(4):
                            bk = cnt["b"] % 4
                            cnt["b"] += 1
                            for k in range(NH):
                                P.op("tensor", lambda e, bk=bk, n=n, k=k, t=t, sl=sl: e.matmul(
                                    psB[bk][:], lhsT=oT[sl][:, k, t * 128:(t + 1) * 128], rhs=wos[n][:, k, :],
                                    start=(k == 0), stop=(k == NH - 1)),
                                    reads=[f"wos{n}", f"oT{sl}"], writes=[f"psB{bk}"], sig=(k == NH - 1))
                            P.op("vector", lambda e, bk=bk, t=t, n=n: e.tensor_tensor(
                                out=xacc[:, t, n * 512:(n + 1) * 512], in0=psB[bk][:], in1=xacc[:, t, n * 512:(n + 1) * 512], op=ALU.add),
                                reads=[f"psB{bk}"], writes=[f"xacc{t}"])
                        P.op("sync", lambda e, t=t, r0=r0: e.dma_start(out=dst[r0 + t * 128:r0 + (t + 1) * 128, :], in_=xacc[:, t, :]),
                             reads=[f"xacc{t}"], writes=[f"dst{r0 + t * 128}"], dsem=f"D_xs{t}")
                        if p + 1 < NPASS:
                            load_x(t, r0 + TT)
                P.barrier()

        oneb_t = st.enter_context(nc.sbuf_tensor("oneb", [128, 1], F32))
        P.op("vector", lambda e: e.memset(oneb_t[:], 1.0), writes=["oneb"])
        Env.oneb = oneb_t
        P.barrier()
        cur = x_in
        nsub = len(plan)
        for si, (kind, idx, gi) in enumerate(plan):
            lastsub = si == nsub - 1
            dst = out if lastsub else xres
            if kind == "mlp":
                phase_mlp(idx, gi, cur, dst, final_norm and lastsub)
            elif kind == "lru":
                phase_lru(idx, gi, cur, dst)
            elif kind == "att":
                phase_att_qkv(idx, gi, cur)
                with contextlib.ExitStack() as shared:
                    wos_res = load_wo(shared, idx)
                    phase_att_core()
                    phase_att_out(idx, cur, dst, wos_res)
            elif kind == "attq":
                phase_att_qkv(idx, gi, cur)
            elif kind == "attqc":
                phase_att_qkv(idx, gi, cur)
                phase_att_core()
            cur = dst
        P.final_waits = {"sync": {s: v for s, v in P.cnt.items() if v > 0}}
        with nc.Block() as block:
            P.emit(block)
    return nc


def full_plan():
    plan = []
    for i in range(4):
        j = i // 2
        plan.append(("lru" if i % 2 == 0 else "att", j, i))
        plan.append(("mlp", i, 4 + i))
    return plan


def host_consts():
    bf = ml_dtypes.bfloat16
    cst = np.zeros((128, 640), np.float32)
    cst[:, 0:128] = np.eye(128)
    cst[:, 128:256] = 1.0
    j = np.arange(128)[:, None]
    i = np.arange(128)[None, :]
    cst[:, 256:384] = (j >= i)
    cst[:, 384:512] = (j <= i)
    for m in range(32):
        if m < 16:
            cst[m + 16, 512 + m] = -1.0
        else:
            cst[m - 16, 512 + m] = 1.0
    inv = 500000.0 ** (-np.arange(0, 32, 2, dtype=np.float32) / 32)
    ang = np.arange(S, dtype=np.float32)[None, :] * np.concatenate([inv, inv]).astype(np.float32)[:, None]
    rope = np.zeros((2, 128, S), np.float32)
    rope[0] = 1.0
    rope[0, 0:32] = np.cos(ang)
    rope[1, 0:32] = np.sin(ang)
    return cst.astype(bf), rope


def prep_shared(inp):
    f = lambda a: np.ascontiguousarray(np.asarray(a, dtype=np.float32))
    gains = np.concatenate([f(inp["mix_norm"]), f(inp["mlp_norm"]), f(inp["final_norm"])[None]], axis=0)
    gains = np.ascontiguousarray(np.broadcast_to(gains[:, None, :], (9, 128, D)))
    vecs = np.concatenate([f(inp["lru_conv_w"]), f(inp["lru_conv_b"])[:, None], f(inp["lru_b_a"]).reshape(2, 1, DR),
                           f(inp["lru_b_x"]).reshape(2, 1, DR), f(inp["lru_lambda"])[:, None]], axis=1)
    lruvec = np.ascontiguousarray(vecs.reshape(2, 8, RC, 128).transpose(0, 3, 1, 2))
    cst, rope = host_consts()
    sh = {"gains": gains, "lruvec": lruvec, "cst": cst, "rope": rope}
    for k in ("mlp_w1", "mlp_w2", "lru_w_in", "lru_w_a", "lru_w_x", "lru_w_out", "attn_w_qkv", "attn_w_o"):
        sh[k] = f(inp[k])
    return sh


def kernel(**inputs):
    n = 8
    x = np.asarray(inputs["x"], dtype=np.float32)
    B = x.shape[0]
    nseq = B // n
    sh = prep_shared(inputs)
    nc = build(nseq, full_plan(), True)
    in_maps = []
    for c in range(n):
        m = dict(sh)
        m["x"] = np.ascontiguousarray(x[c * nseq:(c + 1) * nseq].reshape(nseq * S, D))
        in_maps.append(m)
    res = run_bass_kernel_spmd(nc, in_maps, core_ids=list(range(n)))
    outs = [np.asarray(r["out"], dtype=np.float32).reshape(nseq, S, D) for r in res.results]
    return np.concatenate(outs, axis=0)
```
